# Optimizing a Trainium2 kernel written in Bass

```python
import math
import jax, jax.numpy as jnp
from jax import lax
import numpy as np

D_MODEL = 2048
BATCH = 1
SEQ = 8192
DEPTH = 2
DEC_BATCH = 4
DEC_SEQ = 8192
PAST_LEN = 128

HEAD_DIM = 128
ATTN_HEADS = 8
ATTN_WIDTH = ATTN_HEADS * HEAD_DIM
RNN_HEADS = 8
RNN_BLOCK = 128
RNN_WIDTH = RNN_HEADS * RNN_BLOCK
MIX_WIDTH = ATTN_WIDTH + RNN_WIDTH
IN_WIDTH = 3 * ATTN_WIDTH + 2 * RNN_WIDTH
SPLITS = (ATTN_WIDTH, 2 * ATTN_WIDTH, 3 * ATTN_WIDTH, 3 * ATTN_WIDTH + RNN_WIDTH)
D_FF = 3 * D_MODEL
RNN_CONV = 4
FFN_CONV = 3
RGLRU_C = 8.0
DILATION_PATTERNS = ((128, 1), (512, 4), (2048, 16))
REL_BUCKETS = 32
REL_MAX_DIST = 1024
N_MOD = 6
EPS = 1e-6
NEG_INF = -1e30

kernel_name = "hybrid_dilated_attn_rglru_encoder"


def rms_norm(x, g):
    xf = x.astype(jnp.float32)
    y = xf * lax.rsqrt(jnp.mean(xf * xf, axis=-1, keepdims=True) + EPS)
    return (y * g.astype(jnp.float32)).astype(x.dtype)


def depthwise_conv(x, w, b, left, right):
    S = x.shape[1]
    xp = jnp.pad(x, ((0, 0), (left, right), (0, 0)))
    out = b
    for j in range(w.shape[0]):
        out = out + xp[:, j:j + S] * w[j]
    return out


def t5_bucket(rel):
    nb = REL_BUCKETS // 2
    max_exact = nb // 2
    ret = (rel > 0).astype(jnp.int32) * nb
    n = jnp.abs(rel)
    nf = jnp.maximum(n, 1).astype(jnp.float32)
    large = max_exact + (jnp.log(nf / max_exact) / math.log(REL_MAX_DIST / max_exact)
                         * (nb - max_exact)).astype(jnp.int32)
    large = jnp.minimum(large, nb - 1)
    return ret + jnp.where(n < max_exact, n, large)


def relpos_bias(table, half, dil):
    qi = jnp.arange(half)[:, None]
    kj = jnp.arange(3 * half)[None, :]
    rel = (kj - half - qi) * dil
    return table[t5_bucket(rel)].astype(jnp.float32).transpose(2, 0, 1)


def banded_attention(q, k, v, bias, half):
    n, L, H, Dh = q.shape
    nb = -(-L // half)
    Lp = nb * half

    def pad(t, lo, hi):
        return jnp.pad(t, ((0, 0), (lo, hi), (0, 0), (0, 0)))

    qb = pad(q, 0, Lp - L).reshape(n, nb, half, H, Dh)

    def kblocks(t):
        t = pad(t, half, Lp - L + half).reshape(n, nb + 2, half, H, Dh)
        return jnp.concatenate([t[:, :-2], t[:, 1:-1], t[:, 2:]], axis=2)

    kb, vb = kblocks(k), kblocks(v)
    s = jnp.einsum("nbqhd,nbkhd->nbhqk", qb, kb, preferred_element_type=jnp.float32)
    s = s * (Dh ** -0.5) + bias[None, None]
    qi = jnp.arange(half)[:, None]
    kj = jnp.arange(3 * half)[None, :]
    band = jnp.abs(kj - half - qi) <= half
    kpos = (jnp.arange(nb)[:, None] - 1) * half + kj
    kvalid = (kpos >= 0) & (kpos < L)
    mask = band[None, :, :] & kvalid[:, None, :]
    s = jnp.where(mask[None, :, None], s, NEG_INF)
    m = jnp.max(s, axis=-1, keepdims=True)
    p = jnp.exp(s - m)
    den = jnp.sum(p, axis=-1)
    o = jnp.einsum("nbhqk,nbkhd->nbqhd", p.astype(v.dtype), vb, preferred_element_type=jnp.float32)
    den_t = den.transpose(0, 1, 3, 2)
    o = o / den_t[..., None]
    lse = m[..., 0].transpose(0, 1, 3, 2) + jnp.log(den_t)
    return o.reshape(n, Lp, H, Dh)[:, :L], lse.reshape(n, Lp, H)[:, :L]


def to_strided(t, dil):
    B, S = t.shape[:2]
    return t.reshape(B, S // dil, dil, *t.shape[2:]).swapaxes(1, 2).reshape(B * dil, S // dil, *t.shape[2:])


def from_strided(t, B, dil):
    L = t.shape[1]
    return t.reshape(B, dil, L, *t.shape[2:]).swapaxes(1, 2).reshape(B, dil * L, *t.shape[2:])


def dilated_attention(q, k, v, rel_table):
    B = q.shape[0]
    outs, lses = [], []
    for window, dil in DILATION_PATTERNS:
        half = window // (2 * dil)
        o, lse = banded_attention(to_strided(q, dil), to_strided(k, dil), to_strided(v, dil),
                                  relpos_bias(rel_table, half, dil), half)
        outs.append(from_strided(o, B, dil))
        lses.append(from_strided(lse, B, dil))
    w = jax.nn.softmax(jnp.stack(lses), axis=0)
    return jnp.einsum("pbsh,pbshd->bshd", w, jnp.stack(outs))


def _lin_combine(e1, e2):
    a1, b1 = e1
    a2, b2 = e2
    return a1 * a2, a2 * b1 + b2


def rglru_mixer(xr, gr, conv_w, conv_b, w_rg, b_rg, w_ig, b_ig, lam):
    B, S, R = xr.shape
    u = depthwise_conv(xr, conv_w, conv_b, RNN_CONV // 2, RNN_CONV - 1 - RNN_CONV // 2).astype(jnp.float32)
    ub = u.reshape(B, S, RNN_HEADS, RNN_BLOCK)
    y = jnp.zeros((B, S, R), jnp.float32)
    for d in range(2):
        r = jax.nn.sigmoid(jnp.einsum("bshi,hij->bshj", ub, w_rg[d].astype(jnp.float32)).reshape(B, S, R)
                           + b_rg[d].astype(jnp.float32))
        i = jax.nn.sigmoid(jnp.einsum("bshi,hij->bshj", ub, w_ig[d].astype(jnp.float32)).reshape(B, S, R)
                           + b_ig[d].astype(jnp.float32))
        log_a = -RGLRU_C * r * jax.nn.softplus(-lam[d].astype(jnp.float32))
        a = jnp.exp(log_a)
        bval = jnp.sqrt(-jnp.expm1(2.0 * log_a)) * (i * u)
        if d == 1:
            a, bval = jnp.flip(a, axis=1), jnp.flip(bval, axis=1)
        h = lax.associative_scan(_lin_combine, (a, bval), axis=1)[1]
        if d == 1:
            h = jnp.flip(h, axis=1)
        y = y + h
    return y * jax.nn.gelu(gr.astype(jnp.float32))


def encoder_layer(x, c, l, p):
    dt = x.dtype
    B, S, _ = x.shape
    mod = (jax.nn.silu(c) @ p["w_ada"][l] + p["b_ada"][l])[:, None, :]
    sh1, sc1, g1, sh2, sc2, g2 = jnp.split(mod, N_MOD, axis=-1)
    h = rms_norm(x, p["norm1"][l]) * (1 + sc1) + sh1
    z = h @ p["w_in"][l]
    q, k, v, xr, gr = jnp.split(z, SPLITS, axis=-1)
    heads = lambda t: t.reshape(B, S, ATTN_HEADS, HEAD_DIM)
    att = dilated_attention(heads(q), heads(k), heads(v), p["rel_bias"]).reshape(B, S, ATTN_WIDTH).astype(dt)
    rnn = rglru_mixer(xr, gr, p["rnn_conv_w"][l], p["rnn_conv_b"][l], p["w_rg"][l], p["b_rg"][l],
                      p["w_ig"][l], p["b_ig"][l], p["lam"][l]).astype(dt)
    x = x + g1 * (jnp.concatenate([att, rnn], axis=-1) @ p["w_out"][l])
    h = rms_norm(x, p["norm2"][l]) * (1 + sc2) + sh2
    gate, val = jnp.split(h @ p["w_up"][l], 2, axis=-1)
    gate = depthwise_conv(gate, p["ffn_conv_w"][l], p["ffn_conv_b"][l], FFN_CONV // 2, FFN_CONV // 2)
    x = x + g2 * ((jax.nn.gelu(gate) * val) @ p["w_down"][l])
    return x


def encoder(x, c, p):
    for l in range(DEPTH):
        x = encoder_layer(x, c, l, p)
    return rms_norm(x, p["final_norm"])


def setup_inputs(seed: int = 0) -> dict:
    key = jax.random.key(seed)
    ks = jax.random.split(key, 32)
    f32 = jnp.float32

    def nrm(i, shape, scale):
        return scale * jax.random.normal(ks[i], shape, f32)

    D, L = D_MODEL, DEPTH
    u = jax.random.uniform(ks[20], (L, 2, RNN_WIDTH), f32, minval=0.9, maxval=0.999)
    pa = u ** (1.0 / RGLRU_C)
    lam = jnp.log(pa) - jnp.log1p(-pa)
    return {
        "x_prompt": nrm(0, (BATCH, SEQ, D), 1.0),
        "x_sample": nrm(1, (DEC_BATCH, DEC_SEQ, D), 1.0),
        "c_prompt": nrm(2, (BATCH, D), 1.0),
        "c_sample": nrm(3, (DEC_BATCH, D), 1.0),
        "w_ada": nrm(4, (L, D, N_MOD * D), 0.5 * D ** -0.5),
        "b_ada": nrm(5, (L, N_MOD * D), 0.02),
        "norm1": 1.0 + nrm(6, (L, D), 0.05),
        "w_in": nrm(7, (L, D, IN_WIDTH), D ** -0.5),
        "w_out": nrm(8, (L, MIX_WIDTH, D), MIX_WIDTH ** -0.5),
        "rel_bias": nrm(9, (REL_BUCKETS, ATTN_HEADS), 0.5),
        "rnn_conv_w": nrm(10, (L, RNN_CONV, RNN_WIDTH), RNN_CONV ** -0.5),
        "rnn_conv_b": nrm(11, (L, RNN_WIDTH), 0.01),
        "w_rg": nrm(12, (L, 2, RNN_HEADS, RNN_BLOCK, RNN_BLOCK), RNN_BLOCK ** -0.5),
        "b_rg": nrm(13, (L, 2, RNN_WIDTH), 0.01),
        "w_ig": nrm(14, (L, 2, RNN_HEADS, RNN_BLOCK, RNN_BLOCK), RNN_BLOCK ** -0.5),
        "b_ig": nrm(15, (L, 2, RNN_WIDTH), 0.01),
        "lam": lam,
        "norm2": 1.0 + nrm(16, (L, D), 0.05),
        "w_up": nrm(17, (L, D, 2 * D_FF), D ** -0.5),
        "ffn_conv_w": nrm(18, (L, FFN_CONV, D_FF), FFN_CONV ** -0.5),
        "ffn_conv_b": nrm(19, (L, D_FF), 0.01),
        "w_down": nrm(21, (L, D_FF, D), D_FF ** -0.5),
        "final_norm": 1.0 + nrm(22, (D,), 0.05),
    }


def reference(x_prompt, x_sample, c_prompt, c_sample, w_ada, b_ada, norm1, w_in, w_out, rel_bias,
              rnn_conv_w, rnn_conv_b, w_rg, b_rg, w_ig, b_ig, lam, norm2, w_up, ffn_conv_w, ffn_conv_b,
              w_down, final_norm):
    p = {"w_ada": w_ada, "b_ada": b_ada, "norm1": norm1, "w_in": w_in, "w_out": w_out,
         "rel_bias": rel_bias, "rnn_conv_w": rnn_conv_w, "rnn_conv_b": rnn_conv_b, "w_rg": w_rg,
         "b_rg": b_rg, "w_ig": w_ig, "b_ig": b_ig, "lam": lam, "norm2": norm2, "w_up": w_up,
         "ffn_conv_w": ffn_conv_w, "ffn_conv_b": ffn_conv_b, "w_down": w_down, "final_norm": final_norm}
    y_prompt = encoder(x_prompt, c_prompt, p)
    y_sample = encoder(x_sample, c_sample, p)
    return (y_prompt, y_sample)
```

```python
import math
import os
from contextlib import ExitStack
import numpy as np
import ml_dtypes
import concourse.bass as bass
import concourse.mybir as mybir
from concourse.bass_utils import run_bass_kernel_spmd

F32 = mybir.dt.float32
BF16 = mybir.dt.bfloat16
ALU = mybir.AluOpType
AF = mybir.ActivationFunctionType

D = 2048
KC = 16
NH = 8
DFF = 6144
INW = 5120
EPS = 1e-6
NEG = -30000.0
PATTERNS = ((128, 1), (512, 4), (2048, 16))
GELU_K = 1.5957691216057308


class T_:
    __slots__ = ("w", "r", "dsem", "name", "persist")

    def __init__(self, name="", persist=False):
        self.w = None
        self.r = {}
        self.dsem = None
        self.name = name
        self.persist = persist


class DSem:
    __slots__ = ("h", "count")

    def __init__(self, h):
        self.h = h
        self.count = 0


class Sched:
    ENG = ("pe", "act", "dve", "pool", "sp")

    def __init__(self, nc, stack):
        self.nc = nc
        self.stack = stack
        self.ops = {e: [] for e in self.ENG}
        self.sem = {}
        self.cnt = {e: 0 for e in self.ENG}
        self.known = {e: {} for e in self.ENG}
        for e in self.ENG:
            self.sem[e] = stack.enter_context(nc.semaphore("S_" + e))
        self.nsem = 0
        self.dma_tiles = []
        self.free_ds = []

    def _dsem(self, t):
        if t.dsem is None:
            if self.free_ds:
                t.dsem = self.free_ds.pop()
            else:
                t.dsem = DSem(self.stack.enter_context(self.nc.semaphore("DS%d" % self.nsem)))
                self.nsem += 1
            self.dma_tiles.append(t)
        return t.dsem

    def release(self):
        keep = []
        for t in self.dma_tiles:
            if t.persist:
                keep.append(t)
            else:
                self.free_ds.append(t.dsem)
                t.dsem = None
        self.dma_tiles = keep

    def _wait(self, eng, ev):
        if ev is None:
            return
        sem, val, src = ev
        if src == eng and eng == "pe":
            return
        k = self.known[eng]
        if k.get(id(sem), 0) >= val:
            return
        k[id(sem)] = val
        self.ops[eng].append(lambda e, sem=sem, val=val: e.wait_ge(sem, val))

    def _deps(self, eng, reads, writes):
        for t in reads:
            self._wait(eng, t.w)
        for t in writes:
            self._wait(eng, t.w)
            for ev in t.r.values():
                self._wait(eng, ev)

    def op(self, eng, fn, reads=(), writes=(), inc=True):
        self._deps(eng, reads, writes)
        sem = self.sem[eng]
        ev = (sem, self.cnt[eng] + 1, eng)
        if inc:
            self.cnt[eng] += 1
            self.ops[eng].append(lambda e, fn=fn, sem=sem: fn(e).then_inc(sem, 1))
        else:
            self.ops[eng].append(lambda e, fn=fn: fn(e))
        for t in writes:
            t.w = ev
            t.r = {}
        for t in reads:
            t.r[eng] = ev
        return ev

    def dma(self, out, in_, writes=(), reads=(), q="sp", **kw):
        self._deps(q, reads, writes)
        t = writes[0] if writes else reads[0]
        ds = self._dsem(t)
        ds.count += 16
        sem = ds.h
        ev = (sem, ds.count, "dma")
        self.ops[q].append(
            lambda e, out=out, in_=in_, sem=sem, kw=kw: e.dma_start(out=out, in_=in_, **kw).then_inc(sem, 16))
        for w in writes:
            w.w = ev
            w.r = {}
        for r in reads:
            r.r[("d", id(sem))] = ev
        return ev

    def barrier(self, final=False):
        for e in self.ENG:
            for s in self.ENG:
                if s != e and self.cnt[s] > 0:
                    self._wait(e, (self.sem[s], self.cnt[s], s))
            for t in self.dma_tiles:
                if t.persist and not final:
                    continue
                if t.dsem.count > 0:
                    self._wait(e, (t.dsem.h, t.dsem.count, "dma"))

    def emit(self):
        with self.nc.Block() as block:
            @block.sync
            def _(e):
                for f in self.ops["sp"]:
                    f(e)

            @block.tensor
            def _(e):
                for f in self.ops["pe"]:
                    f(e)

            @block.scalar
            def _(e):
                for f in self.ops["act"]:
                    f(e)

            @block.vector
            def _(e):
                for f in self.ops["dve"]:
                    f(e)

            @block.gpsimd
            def _(e):
                for f in self.ops["pool"]:
                    f(e)


class Arena:
    def __init__(self, ap, words):
        self.a = ap
        self.words = words
        self.off = 0

    def reset(self, to=0):
        self.off = to

    def alloc(self, shape, dt=F32, parts=128):
        n = int(np.prod(shape))
        words = n if dt == F32 else (n + 1) // 2
        words = (words + 7) // 8 * 8
        assert self.off + words <= self.words, ("arena overflow", self.off, words, self.words)
        v = self.a[0:parts, self.off:self.off + words]
        self.off += words
        if dt == BF16:
            v = v.bitcast(BF16)
        v = v[:, 0:n]
        if len(shape) == 2:
            v = v.rearrange("p (a b) -> p a b", a=shape[0])
        elif len(shape) == 3:
            v = v.rearrange("p (a b c) -> p a b c", a=shape[0], b=shape[1])
        elif len(shape) == 4:
            v = v.rearrange("p (a b c d) -> p a b c d", a=shape[0], b=shape[1], c=shape[2])
        return v


def t5_bucket_np(rel):
    nb = 16
    max_exact = 8
    rel = np.asarray(rel, np.int64)
    ret = (rel > 0).astype(np.int64) * nb
    n = np.abs(rel)
    nf = np.maximum(n, 1).astype(np.float32)
    large = max_exact + (np.log(nf / np.float32(max_exact)) / np.float32(math.log(1024 / max_exact))
                         * np.float32(nb - max_exact)).astype(np.int32)
    large = np.minimum(large, nb - 1)
    return ret + np.where(n < max_exact, n, large)


def make_onehot():
    oh = np.zeros((64, 3 * 384), np.float32)
    for p, (win, dil) in enumerate(PATTERNS):
        for e in range(384):
            r = e - 191
            if abs(r) <= 64:
                oh[int(t5_bucket_np(r * dil)), p * 384 + e] = 1.0
            else:
                oh[32, p * 384 + e] = 1.0
    return oh


def build(T=8192, NL=2, dbg=False):
    nc = bass.Bass("TRN2", target_bir_lowering=False)
    NT = T // 128

    def din(name, shape, dt=F32):
        return nc.dram_tensor(name, list(shape), dt, kind="ExternalInput").ap()

    def dscr(name, shape, dt=F32):
        return nc.dram_tensor(name, list(shape), dt, kind=("ExternalOutput" if dbg else "Internal")).ap()

    x_in = din("x", [T, D])
    c_in = din("c", [1, D])
    w_ada = din("w_ada", [2, D, 6 * D])
    b_ada = din("b_ada", [2, 6 * D])
    norm1 = din("norm1", [2, D])
    w_in = din("w_in", [2, D, INW])
    w_out = din("w_out", [2, D, D])
    rel_bias = din("rel_bias", [32, 8])
    rnn_conv_w = din("rnn_conv_w", [2, 4, 1024])
    rnn_conv_b = din("rnn_conv_b", [2, 1024])
    w_rg = din("w_rg", [2, 2, 8, 128, 128])
    b_rg = din("b_rg", [2, 2, 1024])
    w_ig = din("w_ig", [2, 2, 8, 128, 128])
    b_ig = din("b_ig", [2, 2, 1024])
    lam = din("lam", [2, 2, 1024])
    norm2 = din("norm2", [2, D])
    w_up = din("w_up", [2, D, 2 * DFF])
    ffn_conv_w = din("ffn_conv_w", [2, 3, DFF])
    ffn_conv_b = din("ffn_conv_b", [2, DFF])
    w_down = din("w_down", [2, DFF, D])
    final_norm = din("final_norm", [1, D])
    ident_in = din("ident", [128, 128], BF16)
    onehot_in = din("onehot", [64, 3 * 384])
    y_out = nc.dram_tensor("y", [T, D], F32, kind="ExternalOutput").ap()

    win_bf = dscr("win_bf", [NL, 10, 128, KC, 512], BF16)
    wout_bf = dscr("wout_bf", [NL, 128, KC, D], BF16)
    wup_bf = dscr("wup_bf", [NL, 48, 2, 128, KC, 128], BF16)
    wdn_bf = dscr("wdn_bf", [NL, 4, 128, 48, 512], BF16)
    wrg_bf = dscr("wrg_bf", [NL, 2, 8, 128, 128], BF16)
    wig_bf = dscr("wig_bf", [NL, 2, 8, 128, 128], BF16)
    modscr = dscr("modscr", [NL, 6 * D])
    extscr = dscr("extscr", [8, 3 * 384])
    mbscr = dscr("mbscr", [8, 128, 6 * 128])
    qT = dscr("qT", [8, 128, T], BF16)
    kT = dscr("kT", [8, 128, T], BF16)
    vv = dscr("vv", [T, 1024], BF16)
    xrT = dscr("xrT", [8, 128, T])
    grT = dscr("grT", [8, 128, T])
    mixT = dscr("mixT", [16, 128, T], BF16)
    x1s = dscr("x1s", [T, D])
    h2T = dscr("h2T", [16, 128, T], BF16)
    xls = dscr("xls", [T, D])

    with ExitStack() as st:
        S = Sched(nc, st)
        AW = 49152
        arena_t = st.enter_context(nc.sbuf_tensor("arena", [128, AW], F32))
        ps = st.enter_context(nc.psum_tensor("ps", [128, 4096], F32))
        A = Arena(arena_t, AW)

        def bank(b):
            return ps[:, b * 512:(b + 1) * 512]

        ident = A.alloc([128], BF16)
        ones = A.alloc([128], BF16)
        t_ident = T_("ident", persist=True)
        t_ones = T_("ones")
        S.dma(ident, ident_in[:, :], writes=[t_ident])
        S.op("dve", lambda e: e.memset(ones, 1.0), writes=[t_ones])
        A_base = A.off

        t_wc = {}

        def cast_weights(l):
            for nm in ("win", "wrg", "wout", "wup", "wdn"):
                t_wc[(nm, l)] = T_(nm + str(l), persist=True)
            for cb in range(10):
                S.dma(win_bf[l, cb], w_in[l][:, cb * 512:(cb + 1) * 512].rearrange("(kc p) n -> p kc n", p=128),
                      writes=[t_wc[("win", l)]], q="pool")
            S.dma(wrg_bf[l], w_rg[l], writes=[t_wc[("wrg", l)]], q="pool")
            S.dma(wig_bf[l], w_ig[l], writes=[t_wc[("wrg", l)]], q="pool")
            for n4 in range(4):
                S.dma(wout_bf[l][:, :, n4 * 512:(n4 + 1) * 512],
                      w_out[l][:, n4 * 512:(n4 + 1) * 512].rearrange("(kc p) n -> p kc n", p=128),
                      writes=[t_wc[("wout", l)]], q="pool")
            wu = w_up[l].rearrange("(kc p) (gv mc n) -> mc gv p kc n", p=128, gv=2, n=128)
            for mc in range(48):
                for gv in range(2):
                    S.dma(wup_bf[l, mc, gv], wu[mc, gv], writes=[t_wc[("wup", l)]], q="pool")
            for nb in range(4):
                for k3 in range(3):
                    S.dma(wdn_bf[l, nb][:, k3 * 16:(k3 + 1) * 16, :],
                          w_down[l][k3 * 2048:(k3 + 1) * 2048, nb * 512:(nb + 1) * 512].rearrange("(kc p) n -> p kc n", p=128),
                          writes=[t_wc[("wdn", l)]], q="pool")

        import os
        SKIP = os.environ.get("KSKIP", "")
        for l in range(NL):
            if "c" not in SKIP:
                cast_weights(l)

        def prologue():
            A.reset(A_base)
            cT = A.alloc([16])
            scT = A.alloc([16])
            t_c = T_()
            t_sc = T_()
            S.dma(cT, c_in[0, :].rearrange("(kc p) -> p kc", p=128), writes=[t_c], allow_slow_non_contiguous=True)
            S.op("act", lambda e: e.activation(out=scT, in_=cT, func=AF.Silu), reads=[t_c], writes=[t_sc])
            brow = A.alloc([6 * D], parts=1)
            nrow = A.alloc([2 * D], parts=1)
            wt = [A.alloc([KC, 512]) for _ in range(2)]
            t_wt = [T_(), T_()]
            t_brow = T_()
            t_nrow = T_()
            t_pb = [T_(), T_()]
            for l in range(NL if "m" not in SKIP else 0):
                S.dma(brow, b_ada[l:l + 1, :], writes=[t_brow])
                S.dma(nrow[:, 0:D], norm1[l:l + 1, :], writes=[t_nrow])
                S.dma(nrow[:, D:2 * D], norm2[l:l + 1, :], writes=[t_nrow])
                for n in range(24):
                    b = n % 2
                    S.dma(wt[b], w_ada[l][:, n * 512:(n + 1) * 512].rearrange("(kc p) n -> p kc n", p=128), writes=[t_wt[b]])
                    for kc in range(KC):
                        S.op("pe", lambda e, b=b, kc=kc: e.matmul(bank(b)[0:1, :], lhsT=scT[:, kc:kc + 1], rhs=wt[b][:, kc, :],
                                                                   start=(kc == 0), stop=(kc == KC - 1)),
                             reads=[t_sc, t_wt[b]], writes=[t_pb[b]], inc=(kc == KC - 1))
                    S.op("dve", lambda e, b=b, n=n: e.tensor_tensor(out=brow[:, n * 512:(n + 1) * 512], in0=bank(b)[0:1, :],
                                                                     in1=brow[:, n * 512:(n + 1) * 512], op=ALU.add),
                         reads=[t_pb[b]], writes=[t_brow])
                S.op("dve", lambda e: e.scalar_tensor_tensor(out=brow[:, D:2 * D], in0=brow[:, D:2 * D], scalar=1.0,
                                                             in1=nrow[:, 0:D], op0=ALU.add, op1=ALU.mult),
                     reads=[t_nrow], writes=[t_brow])
                S.op("dve", lambda e: e.scalar_tensor_tensor(out=brow[:, 4 * D:5 * D], in0=brow[:, 4 * D:5 * D], scalar=1.0,
                                                             in1=nrow[:, D:2 * D], op0=ALU.add, op1=ALU.mult),
                     reads=[t_nrow], writes=[t_brow])
                S.dma(modscr[l:l + 1, :], brow, reads=[t_brow])
            tab = A.alloc([8], parts=64)
            oh = A.alloc([3 * 384], parts=64)
            ext = A.alloc([3 * 384], parts=8)
            t_tab = T_()
            t_oh = T_()
            t_ext = T_()
            S.op("dve", lambda e: e.memset(tab[32:64, :], NEG), writes=[t_tab])
            S.dma(tab[0:32, :], rel_bias[:, :], writes=[t_tab])
            S.dma(oh, onehot_in[:, :], writes=[t_oh])
            for p in range(3 if "o" not in SKIP else 0):
                S.op("pe", lambda e, p=p: e.matmul(bank(2 + p)[0:8, 0:384], lhsT=tab, rhs=oh[:, p * 384:(p + 1) * 384],
                                                   start=True, stop=True),
                     reads=[t_tab, t_oh], writes=[t_pb[0]])
                S.op("dve", lambda e, p=p: e.tensor_copy(out=ext[:, p * 384:(p + 1) * 384], in_=bank(2 + p)[0:8, 0:384]),
                     reads=[t_pb[0]], writes=[t_ext])
            S.dma(extscr[:, :], ext, reads=[t_ext])
            if os.environ.get("KDEBUG"):
                print("arena words used", A.off, "of", AW)
            S.barrier()
            S.release()
            Hk = [A.alloc([6, 128]) for _ in range(2)]
            Tm = [A.alloc([6, 128]) for _ in range(2)]
            MBt = [A.alloc([6, 128]) for _ in range(2)]
            t_H = [T_(), T_()]
            t_Tm = [T_(), T_()]
            t_MB = [T_(), T_()]
            for hd in range(8 if "h" not in SKIP else 0):
                b = hd % 2
                src = bass.AP(extscr.tensor, hd * 3 * 384, [[1, 128], [128, 6], [1, 128]])
                for p in range(3):
                    srcp = bass.AP(extscr.tensor, hd * 3 * 384 + p * 384, [[1, 128], [128, 2], [1, 128]])
                    S.dma(Hk[b][:, 2 * p:2 * p + 2, :], srcp, writes=[t_H[b]])
                S.op("dve", lambda e, b=b: e.tensor_copy(out=Tm[b], in_=Hk[b][:, :, ::-1]), reads=[t_H[b]], writes=[t_Tm[b]])
                S.op("act", lambda e, b=b: e.activation(out=MBt[b], in_=Tm[b], func=AF.Exp), reads=[t_Tm[b]], writes=[t_MB[b]])
                S.dma(mbscr[hd], MBt[b].rearrange("p a b -> p (a b)"), reads=[t_MB[b]])
            if os.environ.get("KDEBUG"):
                print("arena words used", A.off, "of", AW)
            S.barrier()
            S.release()

        def load_mod_cols(l, idx_gm, idx_sh):
            gmT = A.alloc([16])
            shT = A.alloc([16])
            t_g = T_()
            S.dma(gmT, modscr[l, idx_gm * D:(idx_gm + 1) * D].rearrange("(kc p) -> p kc", p=128), writes=[t_g],
                  allow_slow_non_contiguous=True)
            S.dma(shT, modscr[l, idx_sh * D:(idx_sh + 1) * D].rearrange("(kc p) -> p kc", p=128), writes=[t_g],
                  allow_slow_non_contiguous=True)
            return gmT, shT, t_g

        def bcast_row(dst, src_tensor_ap, offset, t_dst):
            S.dma(dst, bass.AP(src_tensor_ap.tensor, offset, [[0, 128], [1, D]]), writes=[t_dst])

        class NormCtx:
            pass

        def make_norm_ctx():
            n = NormCtx()
            n.sq = A.alloc([D], BF16)
            n.t_sq = T_()
            n.ss = [A.alloc([1]) for _ in range(4)]
            n.rs = [A.alloc([1]) for _ in range(4)]
            n.t_ss = [T_() for _ in range(4)]
            n.t_rs = [T_() for _ in range(4)]
            n.xn = [A.alloc([D], BF16) for _ in range(2)]
            n.t_xn = [T_(), T_()]
            n.t_tp = [T_() for _ in range(4)]
            n.cnt = 0
            return n

        def tp_region(r):
            base = r * 512
            return ps[:, base:base + 256].bitcast(BF16).rearrange("p (a b) -> p a b", a=4)

        def rstd_of(n, src, t_src, k):
            S.op("act", lambda e: e.activation(out=n.sq, in_=src, func=AF.Square, scale=1.0 / math.sqrt(D), accum_out=n.ss[k]),
                 reads=[t_src], writes=[n.t_sq, n.t_ss[k]])
            S.op("act", lambda e: e.activation(out=n.rs[k], in_=n.ss[k], func=AF.Sqrt, bias=EPS, scale=1.0),
                 reads=[n.t_ss[k]], writes=[n.t_rs[k]])
            S.op("dve", lambda e: e.reciprocal(out=n.rs[k], in_=n.rs[k]), reads=[n.t_rs[k]], writes=[n.t_rs[k]])

        def norm_mod_transpose(n, src, t_src, gmT, shT, t_g, dst_fn, t_dst_fn, defer=False):
            k = n.cnt % 4
            b = n.cnt % 2
            n.cnt += 1
            rstd_of(n, src, t_src, k)
            S.op("dve", lambda e: e.tensor_scalar(out=n.xn[b], in0=src, scalar1=n.rs[k], scalar2=None, op0=ALU.mult),
                 reads=[t_src, n.t_rs[k]], writes=[n.t_xn[b]])

            def part2():
                _nmt_part2(n, b, gmT, shT, t_g, dst_fn, t_dst_fn)
            if defer:
                return part2
            part2()
            return None

        def _nmt_part2(n, b, gmT, shT, t_g, dst_fn, t_dst_fn):
            for g4 in range(4):
                r = g4
                reg = tp_region(r)
                for j in range(4):
                    kc = g4 * 4 + j
                    S.op("pe", lambda e, reg=reg, j=j, kc=kc, b=b: e.transpose(reg[:, j, :], n.xn[b][:, kc * 128:(kc + 1) * 128], ident),
                         reads=[n.t_xn[b], t_ident], writes=[n.t_tp[r]], inc=(j == 3))
                for j in range(4):
                    kc = g4 * 4 + j
                    if g4 % 2 == 0:
                        S.op("act", lambda e, reg=reg, j=j, kc=kc: e.activation(out=dst_fn(kc), in_=reg[:, j, :], func=AF.Identity,
                                                                                 scale=gmT[:, kc:kc + 1], bias=shT[:, kc:kc + 1]),
                             reads=[n.t_tp[r], t_g], writes=[t_dst_fn(kc)])
                    else:
                        S.op("dve", lambda e, reg=reg, j=j, kc=kc: e.tensor_scalar(out=dst_fn(kc), in0=reg[:, j, :], scalar1=gmT[:, kc:kc + 1],
                                                                                   scalar2=shT[:, kc:kc + 1], op0=ALU.mult, op1=ALU.add),
                             reads=[n.t_tp[r], t_g], writes=[t_dst_fn(kc)])

        def phase_a(l, xsrc):
            A.reset(A_base)
            TB = 1024 if T >= 1024 else T
            NTI = TB // 128
            NJ = TB // 512
            gmT, shT, t_g = load_mod_cols(l, 1, 0)
            n = make_norm_ctx()
            xt = [A.alloc([D]) for _ in range(2)]
            t_xt = [T_(), T_()]
            hT2 = [A.alloc([KC, TB], BF16) for _ in range(2)]
            t_h2b = [[[T_() for _ in range(NTI)] for _ in range(KC)] for _ in range(2)]
            wb = [A.alloc([KC, 512], BF16) for _ in range(2)]
            t_wb = [T_(), T_()]
            ob = [A.alloc([512]) for _ in range(4)]
            t_ob = [T_() for _ in range(4)]
            t_bk = [T_() for _ in range(4)]
            gcnt = [0]
            scale_q = 1.0 / math.sqrt(128.0)
            wcnt = [0]

            def load_w(cb):
                b = wcnt[0] % 2
                wcnt[0] += 1
                S.dma(wb[b], win_bf[l, cb], writes=[t_wb[b]], reads=[t_wc[("win", l)]])
                return b

            xcnt = [0]

            def norm_tile(tbn, i, defer):
                tok0 = tbn * TB + i * 128
                b = xcnt[0] % 2
                xcnt[0] += 1
                hbuf = hT2[tbn % 2]
                thb = t_h2b[tbn % 2]
                S.dma(xt[b], xsrc[tok0:tok0 + 128, :], writes=[t_xt[b]])
                return norm_mod_transpose(n, xt[b], t_xt[b], gmT, shT, t_g,
                                          lambda kc, i=i, hbuf=hbuf: hbuf[:, kc, i * 128:(i + 1) * 128], lambda kc, i=i, thb=thb: thb[kc][i],
                                          defer=defer)

            NBLK = T // TB
            for i in range(NTI):
                norm_tile(0, i, False)
            for tb in range(NBLK):
                nxt_w = load_w(0)
                hT = hT2[tb % 2]
                t_h = t_h2b[tb % 2]
                pend = None
                cbs = list(range(10))
                for cb in cbs:
                    if pend is not None:
                        pend()
                        pend = None
                    if tb + 1 < NBLK and cb < NTI:
                        pend = norm_tile(tb + 1, cb, True)
                    wbi = nxt_w
                    if cb + 1 < 10:
                        nxt_w = load_w(cb + 1)
                    W = wb[wbi]
                    if cb in (4, 5):
                        for i in range(NTI):
                            tok0 = tb * TB + i * 128
                            g = gcnt[0]
                            gcnt[0] += 1
                            bk = 4 + g % 4
                            for kc in range(KC):
                                S.op("pe", lambda e, bk=bk, kc=kc, i=i, W=W, hT=hT: e.matmul(bank(bk), lhsT=hT[:, kc, i * 128:(i + 1) * 128], rhs=W[:, kc, :],
                                                                                       start=(kc == 0), stop=(kc == KC - 1)),
                                     reads=[t_h[kc][i], t_wb[wbi]], writes=[t_bk[bk - 4]], inc=(kc == KC - 1))
                            o = g % 4
                            obv = ob[o].bitcast(BF16)[:, 0:512]
                            eng = "act" if bk % 2 == 0 else "dve"
                            if eng == "act":
                                S.op("act", lambda e, bk=bk, obv=obv: e.activation(out=obv, in_=bank(bk), func=AF.Copy),
                                     reads=[t_bk[bk - 4]], writes=[t_ob[o]])
                            else:
                                S.op("dve", lambda e, bk=bk, obv=obv: e.tensor_copy(out=obv, in_=bank(bk)),
                                     reads=[t_bk[bk - 4]], writes=[t_ob[o]])
                            S.dma(vv[tok0:tok0 + 128, (cb - 4) * 512:(cb - 3) * 512], obv, reads=[t_ob[o]])
                    else:
                        for m in range(4):
                            f = cb * 512 + m * 128
                            kind = f // 1024
                            hd = (f % 1024) // 128
                            for j in range(NJ):
                                g = gcnt[0]
                                gcnt[0] += 1
                                bk = 4 + g % 4
                                for kc in range(KC):
                                    S.op("pe", lambda e, bk=bk, kc=kc, m=m, j=j, W=W, hT=hT: e.matmul(bank(bk), lhsT=W[:, kc, m * 128:(m + 1) * 128],
                                                                                                rhs=hT[:, kc, j * 512:(j + 1) * 512],
                                                                                                start=(kc == 0), stop=(kc == KC - 1)),
                                         reads=[t_h[kc][j * 4 + q4] for q4 in range(4)] + [t_wb[wbi]], writes=[t_bk[bk - 4]], inc=(kc == KC - 1))
                                o = g % 4
                                tk0 = tb * TB + j * 512
                                if kind <= 1:
                                    obv = ob[o].bitcast(BF16)[:, 0:512]
                                    dst = (qT if kind == 0 else kT)[hd][:, tk0:tk0 + 512]
                                    sc = scale_q if kind == 0 else 1.0
                                else:
                                    obv = ob[o]
                                    dst = (xrT if kind == 3 else grT)[hd][:, tk0:tk0 + 512]
                                    sc = 1.0
                                if bk % 2 == 0:
                                    S.op("act", lambda e, bk=bk, obv=obv, sc=sc: e.activation(out=obv, in_=bank(bk), func=AF.Copy, scale=sc),
                                         reads=[t_bk[bk - 4]], writes=[t_ob[o]])
                                else:
                                    S.op("dve", lambda e, bk=bk, obv=obv, sc=sc: e.tensor_scalar(out=obv, in0=bank(bk), scalar1=sc, scalar2=None, op0=ALU.mult),
                                         reads=[t_bk[bk - 4]], writes=[t_ob[o]])
                                S.dma(dst, obv, reads=[t_ob[o]])
            if os.environ.get("KDEBUG"):
                print("arena words used", A.off, "of", AW)
            S.barrier()
            S.release()

        def phase_attn(l):
            A.reset(A_base)
            NCH = T // 128
            qs2 = [A.alloc([T], BF16) for _ in range(2)]
            ks2 = [A.alloc([T], BF16) for _ in range(2)]
            t_q2 = [T_(), T_()]
            t_k2 = [T_(), T_()]
            Eh2 = [A.alloc([6, 128]) for _ in range(2)]
            t_E2 = [T_(), T_()]
            Vb = [A.alloc([NCH, 128], BF16) for _ in range(2)]
            t_V = [T_(), T_()]
            num = A.alloc([T])
            den = A.alloc([T])
            t_num = T_()
            t_den = T_()
            ao = A.alloc([T], BF16)
            t_ao = T_()
            PT = [A.alloc([2, 128], BF16) for _ in range(4)]
            t_PT = [T_() for _ in range(4)]
            XS = [A.alloc([2, 128]) for _ in range(4)]
            t_XS = [T_() for _ in range(4)]
            t_S = [T_() for _ in range(4)]
            t_O = [T_(), T_()]
            t_Dn = [T_(), T_()]

            def load_head(hd):
                hb = hd % 2
                S.dma(qs2[hb], qT[hd], writes=[t_q2[hb]])
                S.dma(ks2[hb], kT[hd], writes=[t_k2[hb]])
                S.dma(Eh2[hb].rearrange("p a b -> p (a b)"), mbscr[hd], writes=[t_E2[hb]])

            vlist = [(hd, p) for hd in range(8) for p in range(3)]

            def load_v(vi):
                hd, p = vlist[vi]
                dil = PATTERNS[p][1]
                nchr = (T // dil) // 128
                vsrc = bass.AP(vv.tensor, hd * 128, [[dil * 1024, 128], [1024, dil], [128 * dil * 1024, nchr], [1, 128]])
                S.dma(Vb[vi % 2].rearrange("p (r c) d -> p r c d", r=dil), vsrc, writes=[t_V[vi % 2]])

            def Sreg(r):
                base = (0, 1, 6, 7)[r] * 512
                return ps[:, base:base + 256].rearrange("p (a b) -> p a b", a=2)

            tcnt = [0]
            gcnt = [0]

            def sl(j0, n, r, dil):
                s0 = r + dil * j0
                return slice(s0, s0 + dil * (n - 1) + 1, dil)

            def do_tile(hb, p, r, dil, vb, jq0, nq, chunks, mb0, mbcol0, ocol, gb):
                qs, ks, Eh = qs2[hb], ks2[hb], Eh2[hb]
                t_q, t_k, t_E = t_q2[hb], t_k2[hb], t_E2[hb]
                L = T // dil
                nchr = L // 128
                ti = tcnt[0] % 4
                tcnt[0] += 1
                Sr = Sreg(ti)
                qsl = sl(jq0, nq, r, dil)
                ncu = len(chunks)
                for ci, kc_ in enumerate(chunks):
                    ksl = sl(kc_ * 128, 128, r, dil)
                    S.op("pe", lambda e, Sr=Sr, ci=ci, ksl=ksl, qsl=qsl, nq=nq, ks=ks, qs=qs: e.matmul(Sr[:, ci, 0:nq], lhsT=ks[:, ksl], rhs=qs[:, qsl], start=True, stop=True),
                         reads=[t_k, t_q], writes=[t_S[ti]], inc=(ci == ncu - 1))
                S.op("act", lambda e, Sr=Sr, ti=ti, ncu=ncu, nq=nq: e.activation(out=XS[ti][:, 0:ncu, 0:nq], in_=Sr[:, 0:ncu, 0:nq], func=AF.Exp),
                     reads=[t_S[ti]], writes=[t_XS[ti]])
                e0 = p * 2 + mb0
                S.op("pool", lambda e, ti=ti, ncu=ncu, nq=nq, e0=e0, Eh=Eh: e.tensor_tensor(out=PT[ti][:, 0:ncu, 0:nq], in0=XS[ti][:, 0:ncu, 0:nq],
                                                                                         in1=Eh[:, e0:e0 + ncu, mbcol0:mbcol0 + nq], op=ALU.mult),
                     reads=[t_XS[ti], t_E], writes=[t_PT[ti]])
                ob_ = bank(2 + gb)
                db_ = bank(4 + gb)

                def partB():
                    for ci, kc_ in enumerate(chunks):
                        vch = r * nchr + kc_
                        S.op("pe", lambda e, ci=ci, vch=vch: e.matmul(ob_[:, ocol:ocol + nq], lhsT=Vb[vb][:, vch, :], rhs=PT[ti][:, ci, 0:nq],
                                                                       start=(ci == 0), stop=(ci == ncu - 1)),
                             reads=[t_V[vb], t_PT[ti]], writes=[t_O[gb]], inc=False)
                    for ci, kc_ in enumerate(chunks):
                        S.op("pe", lambda e, ci=ci: e.matmul(db_[:, ocol:ocol + nq], lhsT=ones, rhs=PT[ti][:, ci, 0:nq],
                                                              start=(ci == 0), stop=(ci == ncu - 1)),
                             reads=[t_ones, t_PT[ti]], writes=[t_Dn[gb]], inc=(ci == ncu - 1))
                fifo.append(("B", partB))
                drain(2)

            fifo = []

            def drain(keep):
                while sum(1 for k_, _ in fifo if k_ == "B") > keep:
                    k_, f_ = fifo.pop(0)
                    f_()
                    while fifo and fifo[0][0] == "F":
                        fifo.pop(0)[1]()

            def flush(gb, ocol0, nq, jq0, r, dil, first):
                fifo.append(("F", lambda: flush_now(gb, ocol0, nq, jq0, r, dil, first)))

            def flush_now(gb, ocol0, nq, jq0, r, dil, first):
                csl = sl(jq0, nq, r, dil)
                ob_ = bank(2 + gb)
                db_ = bank(4 + gb)
                if first:
                    S.op("dve", lambda e: e.tensor_copy(out=num[:, csl], in_=ob_[:, ocol0:ocol0 + nq]), reads=[t_O[gb]], writes=[t_num])
                    S.op("dve", lambda e: e.tensor_copy(out=den[:, csl], in_=db_[:, ocol0:ocol0 + nq]), reads=[t_Dn[gb]], writes=[t_den])
                else:
                    S.op("dve", lambda e: e.tensor_tensor(out=num[:, csl], in0=ob_[:, ocol0:ocol0 + nq], in1=num[:, csl], op=ALU.add),
                         reads=[t_O[gb]], writes=[t_num])
                    S.op("dve", lambda e: e.tensor_tensor(out=den[:, csl], in0=db_[:, ocol0:ocol0 + nq], in1=den[:, csl], op=ALU.add),
                         reads=[t_Dn[gb]], writes=[t_den])

            load_head(0)
            load_v(0)
            nheads = 8 if os.environ.get("KSTOP", "") not in ("attn1", "rnn1") else 1
            for hd in range(nheads):
                hb = hd % 2
                if hd + 1 < nheads:
                    load_head(hd + 1)
                for p, (win, dil) in enumerate(PATTERNS):
                    vi = hd * 3 + p
                    drain(0)
                    while fifo:
                        fifo.pop(0)[1]()
                    if vi + 1 < nheads * 3:
                        load_v(vi + 1)
                    L = T // dil
                    nchr = L // 128
                    vb = vi % 2
                    first = (p == 0)
                    for r in range(dil):
                        c = 0
                        while c < nchr - 1:
                            gsz = min(4, nchr - 1 - c)
                            gb = gcnt[0] % 2
                            gcnt[0] += 1
                            for gi in range(gsz):
                                cc = c + gi
                                do_tile(hb, p, r, dil, vb, 64 + 128 * cc, 128, [cc, cc + 1], 0, 0, gi * 128, gb)
                            flush(gb, 0, gsz * 128, 64 + 128 * c, r, dil, first)
                            c += gsz
                        gb = gcnt[0] % 2
                        gcnt[0] += 1
                        do_tile(hb, p, r, dil, vb, 0, 64, [0], 1, 64, 0, gb)
                        do_tile(hb, p, r, dil, vb, L - 64, 64, [nchr - 1], 0, 0, 64, gb)
                        flush(gb, 0, 64, 0, r, dil, first)
                        flush(gb, 64, 64, L - 64, r, dil, first)
                drain(0)
                while fifo:
                    fifo.pop(0)[1]()
                S.op("dve", lambda e: e.reciprocal(out=den, in_=den), reads=[t_den], writes=[t_den])
                S.op("dve", lambda e: e.tensor_tensor(out=ao, in0=num, in1=den, op=ALU.mult), reads=[t_num, t_den], writes=[t_ao])
                S.dma(mixT[hd], ao, reads=[t_ao])
            if os.environ.get("KDEBUG"):
                print("arena words used", A.off, "of", AW)
            S.barrier()
            S.release()

        def phase_rnn(l):
            A.reset(A_base)
            CH = min(2048, T)
            NCk = T // CH
            nb4 = CH // 512
            u = A.alloc([T])
            ubf = A.alloc([T], BF16)
            yacc = A.alloc([T])
            t_u = [T_() for _ in range(NCk)]
            t_ub = [T_() for _ in range(NCk)]
            t_y = [T_() for _ in range(NCk)]
            xp2 = [A.alloc([CH + 3]) for _ in range(2)]
            t_xp2 = [T_(), T_()]
            rt2 = [A.alloc([CH]) for _ in range(2)]
            st2 = [A.alloc([CH]) for _ in range(2)]
            it2 = [A.alloc([CH]) for _ in range(2)]
            hs2 = [A.alloc([CH]) for _ in range(2)]
            gr2 = [A.alloc([CH]) for _ in range(2)]
            t_rt2, t_st2, t_it2, t_hs2, t_gr2 = ([T_(), T_()] for _ in range(5))
            obf2 = [A.alloc([CH], BF16) for _ in range(2)]
            t_obf2 = [T_(), T_()]
            wg2 = [A.alloc([4, 128], BF16) for _ in range(2)]
            bg2 = [A.alloc([4]) for _ in range(2)]
            cw2 = [A.alloc([5]) for _ in range(2)]
            lm2 = [A.alloc([2]) for _ in range(2)]
            cl2 = [A.alloc([4]) for _ in range(2)]
            t_wg2 = [T_(), T_()]
            t_sm2 = [T_(), T_()]
            t_cl2 = [T_(), T_()]
            carry = A.alloc([1])
            t_carry = T_()
            t_gb = [T_() for _ in range(8)]
            cctr = [0]
            xctr = [0]
            nheads = 8 if os.environ.get("KSTOP", "") not in ("attn1", "rnn1") else 1

            def load_params(hd):
                hb = hd % 2
                wg, bg, cw, lm, cl = wg2[hb], bg2[hb], cw2[hb], lm2[hb], cl2[hb]
                hsl = slice(hd * 128, (hd + 1) * 128)
                for d in range(2):
                    S.dma(wg[:, d, :], wrg_bf[l, d, hd], writes=[t_wg2[hb]], reads=[t_wc[("wrg", l)]])
                    S.dma(wg[:, 2 + d, :], wig_bf[l, d, hd], writes=[t_wg2[hb]], reads=[t_wc[("wrg", l)]])
                    S.dma(bg[:, d:d + 1], b_rg[l, d, hsl].rearrange("(p o) -> p o", o=1), writes=[t_sm2[hb]])
                    S.dma(bg[:, 2 + d:3 + d], b_ig[l, d, hsl].rearrange("(p o) -> p o", o=1), writes=[t_sm2[hb]])
                    S.dma(lm[:, d:d + 1], lam[l, d, hsl].rearrange("(p o) -> p o", o=1), writes=[t_sm2[hb]])
                for j in range(4):
                    S.dma(cw[:, j:j + 1], rnn_conv_w[l, j, hsl].rearrange("(p o) -> p o", o=1), writes=[t_sm2[hb]])
                S.dma(cw[:, 4:5], rnn_conv_b[l, hsl].rearrange("(p o) -> p o", o=1), writes=[t_sm2[hb]])
                S.op("act", lambda e, cl=cl, lm=lm: e.activation(out=cl[:, 0:2], in_=lm, func=AF.Exp, scale=-1.0), reads=[t_sm2[hb]], writes=[t_cl2[hb]])
                S.op("act", lambda e, cl=cl: e.activation(out=cl[:, 0:2], in_=cl[:, 0:2], func=AF.Ln, bias=1.0, scale=1.0), reads=[t_cl2[hb]], writes=[t_cl2[hb]])
                S.op("dve", lambda e, cl=cl: e.tensor_scalar(out=cl[:, 2:4], in0=cl[:, 0:2], scalar1=-16.0, scalar2=None, op0=ALU.mult), reads=[t_cl2[hb]], writes=[t_cl2[hb]])
                S.op("dve", lambda e, cl=cl: e.tensor_scalar(out=cl[:, 0:2], in0=cl[:, 0:2], scalar1=-8.0, scalar2=None, op0=ALU.mult), reads=[t_cl2[hb]], writes=[t_cl2[hb]])

            load_params(0)
            for hd in range(nheads):
                hb = hd % 2
                wg, bg, cw, cl = wg2[hb], bg2[hb], cw2[hb], cl2[hb]
                t_wg, t_sm, t_cl = t_wg2[hb], t_sm2[hb], t_cl2[hb]
                if hd + 1 < nheads:
                    load_params(hd + 1)
                fwd_first = (hd % 2 == 0)
                dirs = (0, 1) if fwd_first else (1, 0)
                conv_order = list(range(NCk)) if fwd_first else list(range(NCk - 1, -1, -1))
                for ck in conv_order:
                    c0 = ck * CH
                    lo = max(c0 - 2, 0)
                    hi = min(c0 + CH + 1, T)
                    xb = xctr[0] % 2
                    xctr[0] += 1
                    xp = xp2[xb]
                    t_xp = t_xp2[xb]
                    if ck == 0:
                        S.op("dve", lambda e, xp=xp: e.memset(xp[:, 0:2], 0.0), writes=[t_xp])
                    if ck == NCk - 1:
                        S.op("dve", lambda e, xp=xp: e.memset(xp[:, CH + 2:CH + 3], 0.0), writes=[t_xp])
                    S.dma(xp[:, lo - (c0 - 2):hi - (c0 - 2)], xrT[hd][:, lo:hi], writes=[t_xp])
                    uc = u[:, c0:c0 + CH]
                    S.op("dve", lambda e, uc=uc, xp=xp, cw=cw: e.tensor_scalar(out=uc, in0=xp[:, 0:CH], scalar1=cw[:, 0:1], scalar2=cw[:, 4:5], op0=ALU.mult, op1=ALU.add),
                         reads=[t_xp, t_sm], writes=[t_u[ck]])
                    for j in range(1, 4):
                        S.op("dve", lambda e, uc=uc, j=j, xp=xp, cw=cw: e.scalar_tensor_tensor(out=uc, in0=xp[:, j:j + CH], scalar=cw[:, j:j + 1], in1=uc, op0=ALU.mult, op1=ALU.add),
                             reads=[t_xp, t_sm, t_u[ck]], writes=[t_u[ck]])
                    S.op("act", lambda e, uc=uc, c0=c0: e.activation(out=ubf[:, c0:c0 + CH], in_=uc, func=AF.Copy), reads=[t_u[ck]], writes=[t_ub[ck]])
                for si, d in enumerate(dirs):
                    order = range(NCk) if d == 0 else range(NCk - 1, -1, -1)
                    for oi, ck in enumerate(order):
                        c0 = ck * CH
                        pb = cctr[0] % 2
                        cctr[0] += 1
                        rt, s_t, it, hs, grt, obf = rt2[pb], st2[pb], it2[pb], hs2[pb], gr2[pb], obf2[pb]
                        t_rt, t_st, t_it, t_hs, t_gr, t_obf = t_rt2[pb], t_st2[pb], t_it2[pb], t_hs2[pb], t_gr2[pb], t_obf2[pb]
                        if si == 1:
                            S.dma(grt, grT[hd][:, c0:c0 + CH], writes=[t_gr])
                        for gi, (wi, dst, t_dst) in enumerate(((d, rt, t_rt), (2 + d, it, t_it))):
                            for q4 in range(nb4):
                                bk = gi * 4 + q4
                                S.op("pe", lambda e, bk=bk, wi=wi, q4=q4, c0=c0, wg=wg: e.matmul(bank(bk), lhsT=wg[:, wi, :], rhs=ubf[:, c0 + q4 * 512:c0 + (q4 + 1) * 512],
                                                                                               start=True, stop=True),
                                     reads=[t_wg, t_ub[ck]], writes=[t_gb[bk]])
                                S.op("act", lambda e, bk=bk, wi=wi, q4=q4, dst=dst, bg=bg: e.activation(out=dst[:, q4 * 512:(q4 + 1) * 512], in_=bank(bk), func=AF.Sigmoid,
                                                                                                     bias=bg[:, wi:wi + 1], scale=1.0),
                                     reads=[t_gb[bk], t_sm], writes=[t_dst])
                        S.op("act", lambda e, rt=rt, s_t=s_t, d=d, cl=cl: e.activation(out=s_t, in_=rt, func=AF.Exp, scale=cl[:, 2 + d:3 + d]), reads=[t_rt, t_cl], writes=[t_st])
                        S.op("act", lambda e, rt=rt, d=d, cl=cl: e.activation(out=rt, in_=rt, func=AF.Exp, scale=cl[:, d:d + 1]), reads=[t_rt, t_cl], writes=[t_rt])
                        S.op("act", lambda e, s_t=s_t: e.activation(out=s_t, in_=s_t, func=AF.Sqrt, bias=1.0, scale=-1.0), reads=[t_st], writes=[t_st])
                        if si == 1:
                            S.op("act", lambda e, grt=grt: e.activation(out=grt, in_=grt, func=AF.Gelu_apprx_tanh), reads=[t_gr], writes=[t_gr])
                        S.op("dve", lambda e, it=it, c0=c0: e.tensor_tensor(out=it, in0=it, in1=u[:, c0:c0 + CH], op=ALU.mult), reads=[t_it, t_u[ck]], writes=[t_it])
                        S.op("dve", lambda e, it=it, s_t=s_t: e.tensor_tensor(out=it, in0=it, in1=s_t, op=ALU.mult), reads=[t_it, t_st], writes=[t_it])
                        init = 0.0 if oi == 0 else carry
                        yc = yacc[:, c0:c0 + CH]
                        dst_s = yc if si == 0 else hs
                        t_dst_s = t_y[ck] if si == 0 else t_hs
                        if d == 0:
                            S.op("dve", lambda e, rt=rt, it=it, dst_s=dst_s, init=init: e.tensor_tensor_scan(out=dst_s, data0=rt, data1=it, initial=init, op0=ALU.mult, op1=ALU.add),
                                 reads=[t_rt, t_it, t_carry], writes=[t_dst_s])
                            S.op("dve", lambda e, dst_s=dst_s: e.tensor_copy(out=carry, in_=dst_s[:, CH - 1:CH]), reads=[t_dst_s], writes=[t_carry])
                        else:
                            S.op("dve", lambda e, rt=rt, it=it, dst_s=dst_s, init=init: e.tensor_tensor_scan(out=dst_s[:, ::-1], data0=rt[:, ::-1], data1=it[:, ::-1], initial=init,
                                                                                                         op0=ALU.mult, op1=ALU.add),
                                 reads=[t_rt, t_it, t_carry], writes=[t_dst_s])
                            S.op("dve", lambda e, dst_s=dst_s: e.tensor_copy(out=carry, in_=dst_s[:, 0:1]), reads=[t_dst_s], writes=[t_carry])
                        if si == 1:
                            S.op("dve", lambda e, yc=yc, hs=hs: e.tensor_tensor(out=hs, in0=yc, in1=hs, op=ALU.add), reads=[t_hs, t_y[ck]], writes=[t_hs])
                            S.op("dve", lambda e, hs=hs, grt=grt, obf=obf: e.tensor_tensor(out=obf, in0=hs, in1=grt, op=ALU.mult), reads=[t_hs, t_gr], writes=[t_obf])
                            S.dma(mixT[8 + hd][:, c0:c0 + CH], obf, reads=[t_obf])
            if os.environ.get("KDEBUG"):
                print("arena words used", A.off, "of", AW)
            S.barrier()
            S.release()

        def phase_ca(l, xsrc):
            A.reset(A_base)
            TB = 512
            gmT, shT, t_g = load_mod_cols(l, 4, 3)
            n = make_norm_ctx()
            wo = A.alloc([KC, D], BF16)
            t_wo = T_()
            for n4 in range(4):
                S.dma(wo[:, :, n4 * 512:(n4 + 1) * 512], wout_bf[l][:, :, n4 * 512:(n4 + 1) * 512], writes=[t_wo], reads=[t_wc[("wout", l)]])
            g1 = A.alloc([D])
            t_g1 = T_()
            bcast_row(g1, modscr, l * 6 * D + 2 * D, t_g1)
            mx = [A.alloc([KC, TB], BF16) for _ in range(2)]
            t_mx = [T_(), T_()]
            xt = [A.alloc([D]) for _ in range(2)]
            t_xt = [T_(), T_()]
            x1 = [A.alloc([D]) for _ in range(2)]
            t_x1 = [T_(), T_()]
            h2 = [A.alloc([KC, TB], BF16) for _ in range(2)]
            t_h2 = [[[T_() for _ in range(4)] for _ in range(KC)] for _ in range(2)]
            t_bk = [T_() for _ in range(4)]
            NB = T // TB
            S.dma(mx[0], mixT[:, :, 0:TB].rearrange("k p t -> p k t"), writes=[t_mx[0]])
            cnt = 0
            pend = [None]
            for tb in range(NB):
                mb = tb % 2
                if tb + 1 < NB:
                    S.dma(mx[1 - mb], mixT[:, :, (tb + 1) * TB:(tb + 2) * TB].rearrange("k p t -> p k t"), writes=[t_mx[1 - mb]])
                for i in range(4):
                    tok0 = tb * TB + i * 128
                    b = cnt % 2
                    cnt += 1
                    S.dma(xt[b], xsrc[tok0:tok0 + 128, :], writes=[t_xt[b]])
                    for n4 in range(4):
                        if n4 == 2 and pend[0] is not None:
                            pend[0]()
                            pend[0] = None
                        bk = 4 + n4
                        for kc in range(KC):
                            S.op("pe", lambda e, bk=bk, kc=kc, i=i, mb=mb, n4=n4: e.matmul(bank(bk), lhsT=mx[mb][:, kc, i * 128:(i + 1) * 128],
                                                                                            rhs=wo[:, kc, n4 * 512:(n4 + 1) * 512],
                                                                                            start=(kc == 0), stop=(kc == KC - 1)),
                                 reads=[t_mx[mb], t_wo], writes=[t_bk[n4]], inc=(kc == KC - 1))
                        S.op("dve", lambda e, bk=bk, n4=n4, b=b: e.tensor_tensor(out=x1[b][:, n4 * 512:(n4 + 1) * 512], in0=bank(bk),
                                                                                  in1=g1[:, n4 * 512:(n4 + 1) * 512], op=ALU.mult),
                             reads=[t_bk[n4], t_g1], writes=[t_x1[b]])
                    S.op("pool", lambda e, b=b: e.tensor_tensor(out=x1[b], in0=x1[b], in1=xt[b], op=ALU.add), reads=[t_x1[b], t_xt[b]], writes=[t_x1[b]])
                    S.dma(x1s[tok0:tok0 + 128, :], x1[b], reads=[t_x1[b]])
                    pend[0] = norm_mod_transpose(n, x1[b], t_x1[b], gmT, shT, t_g,
                                                 lambda kc, i=i, mb=mb: h2[mb][:, kc, i * 128:(i + 1) * 128], lambda kc, i=i, mb=mb: t_h2[mb][kc][i],
                                                 defer=(i < 3))
                S.dma(h2T[:, :, tb * TB:(tb + 1) * TB].rearrange("k p t -> p k t"), h2[mb],
                      reads=[t_h2[mb][kc][i] for kc in range(KC) for i in range(4)])
            if os.environ.get("KDEBUG"):
                print("arena words used", A.off, "of", AW)
            S.barrier()
            S.release()

        def phase_cb(l, last):
            A.reset(A_base)
            TB = 512
            NB = T // TB
            g2 = A.alloc([D])
            t_g2 = T_()
            bcast_row(g2, modscr, l * 6 * D + 5 * D, t_g2)
            if last:
                fn = A.alloc([D])
                t_fn = T_()
                bcast_row(fn, final_norm, 0, t_fn)
                n = make_norm_ctx()
            fw = A.alloc([4, 48])
            t_fw = T_()
            for j in range(3):
                S.dma(fw[:, j, :], ffn_conv_w[l, j, :].rearrange("(mc p) -> p mc", p=128), writes=[t_fw], allow_slow_non_contiguous=True)
            S.dma(fw[:, 3, :], ffn_conv_b[l, :].rearrange("(mc p) -> p mc", p=128), writes=[t_fw], allow_slow_non_contiguous=True)
            hb = A.alloc([KC, TB], BF16)
            t_hb = T_()
            halo = A.alloc([KC, 2], BF16)
            t_halo = T_()
            wgv = [A.alloc([2, KC, 128], BF16) for _ in range(2)]
            t_wgv = [T_(), T_()]
            actT = A.alloc([48, TB], BF16)
            t_act = [T_() for _ in range(48)]
            gsb = [A.alloc([TB + 2]) for _ in range(2)]
            t_gsb = [T_(), T_()]
            cac = [A.alloc([TB]) for _ in range(2)]
            t_cac = [T_(), T_()]
            sg = [A.alloc([TB]) for _ in range(2)]
            t_sg = [T_(), T_()]
            vsb = [A.alloc([TB]) for _ in range(2)]
            t_vsb = [T_(), T_()]
            wd = [A.alloc([16, 512], BF16) for _ in range(2)]
            t_wd = [T_(), T_()]
            xr_ = [A.alloc([D]) for _ in range(4)]
            t_xr = [T_() for _ in range(4)]
            t_G = [T_(), T_()]
            t_Vp = T_()
            t_Hh = [T_(), T_()]
            t_acc = [T_() for _ in range(4)]
            tmp = sg
            t_tmp = t_sg
            wcnt = 0
            dcnt = 0
            ecnt = 0
            for tb in range(NB):
                tok0 = tb * TB
                S.dma(hb, h2T[:, :, tok0:tok0 + TB].rearrange("k p t -> p k t"), writes=[t_hb])
                if tb == 0:
                    S.op("dve", lambda e: e.memset(halo[:, :, 0:1], 0.0), writes=[t_halo])
                else:
                    S.dma(halo[:, :, 0:1], h2T[:, :, tok0 - 1:tok0].rearrange("k p t -> p k t"), writes=[t_halo], allow_slow_non_contiguous=True)
                if tb == NB - 1:
                    S.op("dve", lambda e: e.memset(halo[:, :, 1:2], 0.0), writes=[t_halo])
                else:
                    S.dma(halo[:, :, 1:2], h2T[:, :, tok0 + TB:tok0 + TB + 1].rearrange("k p t -> p k t"), writes=[t_halo], allow_slow_non_contiguous=True)
                for i in range(4):
                    S.dma(xr_[i], x1s[tok0 + i * 128:tok0 + (i + 1) * 128, :], writes=[t_xr[i]])
                for mc in range(48):
                    wbi = wcnt % 2
                    wcnt += 1
                    S.dma(wgv[wbi], wup_bf[l, mc].rearrange("g p k n -> p g k n"), writes=[t_wgv[wbi]], reads=[t_wc[("wup", l)]])
                    gb = mc % 2
                    Gb = bank(4 + gb)
                    Vp = bank(6)
                    Hh = bank(7)[:, 0:2]
                    for kc in range(KC):
                        S.op("pe", lambda e, Gb=Gb, kc=kc, wbi=wbi: e.matmul(Gb, lhsT=wgv[wbi][:, 0, kc, :], rhs=hb[:, kc, :], start=(kc == 0), stop=(kc == KC - 1)),
                             reads=[t_wgv[wbi], t_hb], writes=[t_G[gb]], inc=(kc == KC - 1))
                    for kc in range(KC):
                        S.op("pe", lambda e, Hh=Hh, kc=kc, wbi=wbi: e.matmul(Hh, lhsT=wgv[wbi][:, 0, kc, :], rhs=halo[:, kc, :], start=(kc == 0), stop=(kc == KC - 1)),
                             reads=[t_wgv[wbi], t_halo], writes=[t_Hh[0]], inc=(kc == KC - 1))
                    for kc in range(KC):
                        S.op("pe", lambda e, Vp=Vp, kc=kc, wbi=wbi: e.matmul(Vp, lhsT=wgv[wbi][:, 1, kc, :], rhs=hb[:, kc, :], start=(kc == 0), stop=(kc == KC - 1)),
                             reads=[t_wgv[wbi], t_hb], writes=[t_Vp], inc=(kc == KC - 1))
                    eb = ecnt % 2
                    ecnt += 1
                    S.op("act", lambda e, eb=eb, Gb=Gb: e.activation(out=gsb[eb][:, 1:TB + 1], in_=Gb, func=AF.Copy), reads=[t_G[gb]], writes=[t_gsb[eb]])
                    S.op("act", lambda e, eb=eb, Hh=Hh: e.activation(out=gsb[eb][:, 0:1], in_=Hh[:, 0:1], func=AF.Copy), reads=[t_Hh[0]], writes=[t_gsb[eb]])
                    S.op("act", lambda e, eb=eb, Hh=Hh: e.activation(out=gsb[eb][:, TB + 1:TB + 2], in_=Hh[:, 1:2], func=AF.Copy), reads=[t_Hh[0]], writes=[t_gsb[eb]])
                    S.op("dve", lambda e, eb=eb, mc=mc: e.tensor_scalar(out=cac[eb], in0=gsb[eb][:, 0:TB], scalar1=fw[:, 0, mc:mc + 1], scalar2=fw[:, 3, mc:mc + 1],
                                                                        op0=ALU.mult, op1=ALU.add),
                         reads=[t_gsb[eb], t_fw], writes=[t_cac[eb]])
                    for j in (1, 2):
                        S.op("dve", lambda e, eb=eb, mc=mc, j=j: e.scalar_tensor_tensor(out=cac[eb], in0=gsb[eb][:, j:j + TB], scalar=fw[:, j, mc:mc + 1], in1=cac[eb],
                                                                                        op0=ALU.mult, op1=ALU.add),
                             reads=[t_gsb[eb], t_fw, t_cac[eb]], writes=[t_cac[eb]])
                    S.op("act", lambda e, eb=eb: e.activation(out=sg[eb], in_=cac[eb], func=AF.Gelu_apprx_tanh), reads=[t_cac[eb]], writes=[t_sg[eb]])
                    S.op("act", lambda e, eb=eb, Vp=Vp: e.activation(out=vsb[eb], in_=Vp, func=AF.Copy), reads=[t_Vp], writes=[t_vsb[eb]])
                    S.op("dve", lambda e, eb=eb, mc=mc: e.tensor_tensor(out=actT[:, mc, :], in0=vsb[eb], in1=sg[eb], op=ALU.mult),
                         reads=[t_vsb[eb], t_sg[eb]], writes=[t_act[mc]])
                for nb in range(4):
                    for k3 in range(3):
                        wdi = dcnt % 2
                        dcnt += 1
                        S.dma(wd[wdi], wdn_bf[l, nb][:, k3 * 16:(k3 + 1) * 16, :], writes=[t_wd[wdi]], reads=[t_wc[("wdn", l)]])
                        for i in range(4):
                            for kk in range(16):
                                kc = k3 * 16 + kk
                                S.op("pe", lambda e, i=i, kc=kc, kk=kk, wdi=wdi: e.matmul(bank(i), lhsT=actT[:, kc, i * 128:(i + 1) * 128], rhs=wd[wdi][:, kk, :],
                                                                                          start=(kc == 0), stop=(kc == 47)),
                                     reads=[t_act[kc], t_wd[wdi]], writes=[t_acc[i]], inc=(kk == 15))
                    for i in range(4):
                        tq = (nb * 4 + i) % 2
                        S.op("dve", lambda e, i=i, nb=nb, tq=tq: e.tensor_tensor(out=tmp[tq], in0=bank(i), in1=g2[:, nb * 512:(nb + 1) * 512], op=ALU.mult),
                             reads=[t_acc[i], t_g2], writes=[t_tmp[tq]])
                        S.op("pool", lambda e, i=i, nb=nb, tq=tq: e.tensor_tensor(out=xr_[i][:, nb * 512:(nb + 1) * 512], in0=xr_[i][:, nb * 512:(nb + 1) * 512],
                                                                                   in1=tmp[tq], op=ALU.add),
                             reads=[t_tmp[tq], t_xr[i]], writes=[t_xr[i]])
                for i in range(4):
                    r0 = tok0 + i * 128
                    if last:
                        k = n.cnt % 4
                        n.cnt += 1
                        rstd_of(n, xr_[i], t_xr[i], k)
                        S.op("dve", lambda e, i=i, k=k: e.scalar_tensor_tensor(out=xr_[i], in0=xr_[i], scalar=n.rs[k], in1=fn, op0=ALU.mult, op1=ALU.mult),
                             reads=[t_xr[i], n.t_rs[k], t_fn], writes=[t_xr[i]])
                        S.dma(y_out[r0:r0 + 128, :], xr_[i], reads=[t_xr[i]])
                    else:
                        S.dma(xls[r0:r0 + 128, :], xr_[i], reads=[t_xr[i]])
            if os.environ.get("KDEBUG"):
                print("arena words used", A.off, "of", AW)
            S.barrier()
            S.release()

        import os
        stop = os.environ.get("KSTOP", "")
        prologue()
        for l in range(NL):
            if stop == "pro":
                break
            xsrc = x_in if l == 0 else xls
            phase_a(l, xsrc)
            if stop == "a":
                break
            phase_attn(l)
            if stop == "attn1":
                break
            phase_rnn(l)
            if stop in ("attn1", "rnn1", "b"):
                break
            phase_ca(l, xsrc)
            if stop == "ca":
                break
            phase_cb(l, l == NL - 1)
        S.barrier(final=True)
        S.emit()
    return nc


_CACHE = {}


def _consts():
    return {"ident": np.eye(128, dtype=np.float32).astype(ml_dtypes.bfloat16), "onehot": make_onehot()}


def make_in_map(x_seq, c_seq, weights):
    m = dict(weights)
    m["x"] = np.ascontiguousarray(x_seq, dtype=np.float32)
    m["c"] = np.ascontiguousarray(c_seq, dtype=np.float32).reshape(1, D)
    m.update(_consts())
    return m


def kernel(x_prompt, x_sample, c_prompt, c_sample, w_ada, b_ada, norm1, w_in, w_out, rel_bias,
           rnn_conv_w, rnn_conv_b, w_rg, b_rg, w_ig, b_ig, lam, norm2, w_up, ffn_conv_w, ffn_conv_b,
           w_down, final_norm):
    T = x_prompt.shape[1]
    weights = {"w_ada": w_ada, "b_ada": b_ada, "norm1": norm1, "w_in": w_in, "w_out": w_out, "rel_bias": rel_bias,
               "rnn_conv_w": rnn_conv_w, "rnn_conv_b": rnn_conv_b, "w_rg": w_rg, "b_rg": b_rg, "w_ig": w_ig,
               "b_ig": b_ig, "lam": lam, "norm2": norm2, "w_up": w_up, "ffn_conv_w": ffn_conv_w,
               "ffn_conv_b": ffn_conv_b, "w_down": w_down, "final_norm": np.asarray(final_norm).reshape(1, D)}
    weights = {k: np.ascontiguousarray(np.asarray(v), dtype=np.float32) for k, v in weights.items()}
    xs = [np.asarray(x_prompt)[0]] + [np.asarray(x_sample)[i] for i in range(4)]
    cs = [np.asarray(c_prompt)[0]] + [np.asarray(c_sample)[i] for i in range(4)]
    seq_of_core = [0, 1, 2, 3, 4, 0, 1, 2]
    if "nc" not in _CACHE:
        _CACHE["nc"] = build(T=T, NL=2)
    nc = _CACHE["nc"]
    in_maps = [make_in_map(xs[s], cs[s], weights) for s in seq_of_core]
    res = run_bass_kernel_spmd(nc, in_maps, core_ids=list(range(8)))
    outs = [np.asarray(res.results[i]["y"], dtype=np.float32) for i in range(5)]
    y_prompt = outs[0][None]
    y_sample = np.stack(outs[1:5], axis=0)
    return (y_prompt, y_sample)
```

```python
import math
import os
from contextlib import ExitStack
import numpy as np
import ml_dtypes
import concourse.bass as bass
import concourse.mybir as mybir
from concourse.bass_utils import run_bass_kernel_spmd

F32 = mybir.dt.float32
BF16 = mybir.dt.bfloat16
ALU = mybir.AluOpType
AF = mybir.ActivationFunctionType

D = 2048
KC = 16
NH = 8
DFF = 6144
INW = 5120
EPS = 1e-6
NEG = -30000.0
PATTERNS = ((128, 1), (512, 4), (2048, 16))
GELU_K = 1.5957691216057308


class T_:
    __slots__ = ("w", "r", "dsem", "name", "persist")

    def __init__(self, name="", persist=False):
        self.w = None
        self.r = {}
        self.dsem = None
        self.name = name
        self.persist = persist


class DSem:
    __slots__ = ("h", "count")

    def __init__(self, h):
        self.h = h
        self.count = 0


class Sched:
    ENG = ("pe", "act", "dve", "pool", "sp")

    def __init__(self, nc, stack):
        self.nc = nc
        self.stack = stack
        self.ops = {e: [] for e in self.ENG}
        self.sem = {}
        self.cnt = {e: 0 for e in self.ENG}
        self.known = {e: {} for e in self.ENG}
        for e in self.ENG:
            self.sem[e] = stack.enter_context(nc.semaphore("S_" + e))
        self.nsem = 0
        self.dma_tiles = []
        self.free_ds = []

    def _dsem(self, t):
        if t.dsem is None:
            if self.free_ds:
                t.dsem = self.free_ds.pop()
            else:
                t.dsem = DSem(self.stack.enter_context(self.nc.semaphore("DS%d" % self.nsem)))
                self.nsem += 1
            self.dma_tiles.append(t)
        return t.dsem

    def release(self):
        keep = []
        for t in self.dma_tiles:
            if t.persist:
                keep.append(t)
            else:
                self.free_ds.append(t.dsem)
                t.dsem = None
        self.dma_tiles = keep

    def _wait(self, eng, ev):
        if ev is None:
            return
        sem, val, src = ev
        if src == eng and eng == "pe":
            return
        k = self.known[eng]
        if k.get(id(sem), 0) >= val:
            return
        k[id(sem)] = val
        self.ops[eng].append(lambda e, sem=sem, val=val: e.wait_ge(sem, val))

    def _deps(self, eng, reads, writes):
        for t in reads:
            self._wait(eng, t.w)
        for t in writes:
            self._wait(eng, t.w)
            for ev in t.r.values():
                self._wait(eng, ev)

    def op(self, eng, fn, reads=(), writes=(), inc=True):
        self._deps(eng, reads, writes)
        sem = self.sem[eng]
        ev = (sem, self.cnt[eng] + 1, eng)
        if inc:
            self.cnt[eng] += 1
            self.ops[eng].append(lambda e, fn=fn, sem=sem: fn(e).then_inc(sem, 1))
        else:
            self.ops[eng].append(lambda e, fn=fn: fn(e))
        for t in writes:
            t.w = ev
            t.r = {}
        for t in reads:
            t.r[eng] = ev
        return ev

    def dma(self, out, in_, writes=(), reads=(), q="sp", **kw):
        self._deps(q, reads, writes)
        t = writes[0] if writes else reads[0]
        ds = self._dsem(t)
        ds.count += 16
        sem = ds.h
        ev = (sem, ds.count, "dma")
        self.ops[q].append(
            lambda e, out=out, in_=in_, sem=sem, kw=kw: e.dma_start(out=out, in_=in_, **kw).then_inc(sem, 16))
        for w in writes:
            w.w = ev
            w.r = {}
        for r in reads:
            r.r[("d", id(sem))] = ev
        return ev

    def barrier(self, final=False):
        for e in self.ENG:
            for s in self.ENG:
                if s != e and self.cnt[s] > 0:
                    self._wait(e, (self.sem[s], self.cnt[s], s))
            for t in self.dma_tiles:
                if t.persist and not final:
                    continue
                if t.dsem.count > 0:
                    self._wait(e, (t.dsem.h, t.dsem.count, "dma"))

    def emit(self):
        with self.nc.Block() as block:
            @block.sync
            def _(e):
                for f in self.ops["sp"]:
                    f(e)

            @block.tensor
            def _(e):
                for f in self.ops["pe"]:
                    f(e)

            @block.scalar
            def _(e):
                for f in self.ops["act"]:
                    f(e)

            @block.vector
            def _(e):
                for f in self.ops["dve"]:
                    f(e)

            @block.gpsimd
            def _(e):
                for f in self.ops["pool"]:
                    f(e)


class Arena:
    def __init__(self, ap, words):
        self.a = ap
        self.words = words
        self.off = 0

    def reset(self, to=0):
        self.off = to

    def alloc(self, shape, dt=F32, parts=128):
        n = int(np.prod(shape))
        words = n if dt == F32 else (n + 1) // 2
        words = (words + 7) // 8 * 8
        assert self.off + words <= self.words, ("arena overflow", self.off, words, self.words)
        v = self.a[0:parts, self.off:self.off + words]
        self.off += words
        if dt == BF16:
            v = v.bitcast(BF16)
        v = v[:, 0:n]
        if len(shape) == 2:
            v = v.rearrange("p (a b) -> p a b", a=shape[0])
        elif len(shape) == 3:
            v = v.rearrange("p (a b c) -> p a b c", a=shape[0], b=shape[1])
        elif len(shape) == 4:
            v = v.rearrange("p (a b c d) -> p a b c d", a=shape[0], b=shape[1], c=shape[2])
        return v


def t5_bucket_np(rel):
    nb = 16
    max_exact = 8
    rel = np.asarray(rel, np.int64)
    ret = (rel > 0).astype(np.int64) * nb
    n = np.abs(rel)
    nf = np.maximum(n, 1).astype(np.float32)
    large = max_exact + (np.log(nf / np.float32(max_exact)) / np.float32(math.log(1024 / max_exact))
                         * np.float32(nb - max_exact)).astype(np.int32)
    large = np.minimum(large, nb - 1)
    return ret + np.where(n < max_exact, n, large)


def make_onehot():
    oh = np.zeros((64, 3 * 384), np.float32)
    for p, (win, dil) in enumerate(PATTERNS):
        for e in range(384):
            r = e - 191
            if abs(r) <= 64:
                oh[int(t5_bucket_np(r * dil)), p * 384 + e] = 1.0
            else:
                oh[32, p * 384 + e] = 1.0
    return oh


def build(T=8192, NL=2, dbg=False):
    nc = bass.Bass("TRN2", target_bir_lowering=False)
    NT = T // 128

    def din(name, shape, dt=F32):
        return nc.dram_tensor(name, list(shape), dt, kind="ExternalInput").ap()

    def dscr(name, shape, dt=F32):
        return nc.dram_tensor(name, list(shape), dt, kind=("ExternalOutput" if dbg else "Internal")).ap()

    x_in = din("x", [T, D])
    c_in = din("c", [1, D])
    w_ada = din("w_ada", [2, D, 6 * D])
    b_ada = din("b_ada", [2, 6 * D])
    norm1 = din("norm1", [2, D])
    w_in = din("w_in", [2, D, INW])
    w_out = din("w_out", [2, D, D])
    rel_bias = din("rel_bias", [32, 8])
    rnn_conv_w = din("rnn_conv_w", [2, 4, 1024])
    rnn_conv_b = din("rnn_conv_b", [2, 1024])
    w_rg = din("w_rg", [2, 2, 8, 128, 128])
    b_rg = din("b_rg", [2, 2, 1024])
    w_ig = din("w_ig", [2, 2, 8, 128, 128])
    b_ig = din("b_ig", [2, 2, 1024])
    lam = din("lam", [2, 2, 1024])
    norm2 = din("norm2", [2, D])
    w_up = din("w_up", [2, D, 2 * DFF])
    ffn_conv_w = din("ffn_conv_w", [2, 3, DFF])
    ffn_conv_b = din("ffn_conv_b", [2, DFF])
    w_down = din("w_down", [2, DFF, D])
    final_norm = din("final_norm", [1, D])
    ident_in = din("ident", [128, 128], BF16)
    onehot_in = din("onehot", [64, 3 * 384])
    y_out = nc.dram_tensor("y", [T, D], F32, kind="ExternalOutput").ap()

    win_bf = dscr("win_bf", [NL, 10, 128, KC, 512], BF16)
    wout_bf = dscr("wout_bf", [NL, 128, KC, D], BF16)
    wup_bf = dscr("wup_bf", [NL, 48, 2, 128, KC, 128], BF16)
    wdn_bf = dscr("wdn_bf", [NL, 4, 128, 48, 512], BF16)
    wrg_bf = dscr("wrg_bf", [NL, 2, 8, 128, 128], BF16)
    wig_bf = dscr("wig_bf", [NL, 2, 8, 128, 128], BF16)
    modscr = dscr("modscr", [NL, 6 * D])
    extscr = dscr("extscr", [8, 3 * 384])
    mbscr = dscr("mbscr", [8, 128, 6 * 128])
    qT = dscr("qT", [8, 128, T], BF16)
    kT = dscr("kT", [8, 128, T], BF16)
    vv = dscr("vv", [T, 1024], BF16)
    xrT = dscr("xrT", [8, 128, T])
    grT = dscr("grT", [8, 128, T])
    mixT = dscr("mixT", [16, 128, T], BF16)
    x1s = dscr("x1s", [T, D])
    h2T = dscr("h2T", [16, 128, T], BF16)
    xls = dscr("xls", [T, D])

    with ExitStack() as st:
        S = Sched(nc, st)
        AW = 49152
        arena_t = st.enter_context(nc.sbuf_tensor("arena", [128, AW], F32))
        ps = st.enter_context(nc.psum_tensor("ps", [128, 4096], F32))
        A = Arena(arena_t, AW)

        def bank(b):
            return ps[:, b * 512:(b + 1) * 512]

        ident = A.alloc([128], BF16)
        ones = A.alloc([128], BF16)
        t_ident = T_("ident", persist=True)
        t_ones = T_("ones")
        S.dma(ident, ident_in[:, :], writes=[t_ident])
        S.op("dve", lambda e: e.memset(ones, 1.0), writes=[t_ones])
        A_base = A.off

        t_wc = {}

        def cast_weights(l):
            for nm in ("win", "wrg", "wout", "wup", "wdn"):
                t_wc[(nm, l)] = T_(nm + str(l), persist=True)
            for cb in range(10):
                S.dma(win_bf[l, cb], w_in[l][:, cb * 512:(cb + 1) * 512].rearrange("(kc p) n -> p kc n", p=128),
                      writes=[t_wc[("win", l)]], q="pool")
            S.dma(wrg_bf[l], w_rg[l], writes=[t_wc[("wrg", l)]], q="pool")
            S.dma(wig_bf[l], w_ig[l], writes=[t_wc[("wrg", l)]], q="pool")
            for n4 in range(4):
                S.dma(wout_bf[l][:, :, n4 * 512:(n4 + 1) * 512],
                      w_out[l][:, n4 * 512:(n4 + 1) * 512].rearrange("(kc p) n -> p kc n", p=128),
                      writes=[t_wc[("wout", l)]], q="pool")
            wu = w_up[l].rearrange("(kc p) (gv mc n) -> mc gv p kc n", p=128, gv=2, n=128)
            for mc in range(48):
                for gv in range(2):
                    S.dma(wup_bf[l, mc, gv], wu[mc, gv], writes=[t_wc[("wup", l)]], q="pool")
            for nb in range(4):
                for k3 in range(3):
                    S.dma(wdn_bf[l, nb][:, k3 * 16:(k3 + 1) * 16, :],
                          w_down[l][k3 * 2048:(k3 + 1) * 2048, nb * 512:(nb + 1) * 512].rearrange("(kc p) n -> p kc n", p=128),
                          writes=[t_wc[("wdn", l)]], q="pool")

        import os
        SKIP = os.environ.get("KSKIP", "")
        for l in range(NL):
            if "c" not in SKIP:
                cast_weights(l)

        def prologue():
            A.reset(A_base)
            cT = A.alloc([16])
            scT = A.alloc([16])
            t_c = T_()
            t_sc = T_()
            S.dma(cT, c_in[0, :].rearrange("(kc p) -> p kc", p=128), writes=[t_c], allow_slow_non_contiguous=True)
            S.op("act", lambda e: e.activation(out=scT, in_=cT, func=AF.Silu), reads=[t_c], writes=[t_sc])
            brow = A.alloc([6 * D], parts=1)
            nrow = A.alloc([2 * D], parts=1)
            wt = [A.alloc([KC, 512]) for _ in range(2)]
            t_wt = [T_(), T_()]
            t_brow = T_()
            t_nrow = T_()
            t_pb = [T_(), T_()]
            for l in range(NL if "m" not in SKIP else 0):
                S.dma(brow, b_ada[l:l + 1, :], writes=[t_brow])
                S.dma(nrow[:, 0:D], norm1[l:l + 1, :], writes=[t_nrow])
                S.dma(nrow[:, D:2 * D], norm2[l:l + 1, :], writes=[t_nrow])
                for n in range(24):
                    b = n % 2
                    S.dma(wt[b], w_ada[l][:, n * 512:(n + 1) * 512].rearrange("(kc p) n -> p kc n", p=128), writes=[t_wt[b]])
                    for kc in range(KC):
                        S.op("pe", lambda e, b=b, kc=kc: e.matmul(bank(b)[0:1, :], lhsT=scT[:, kc:kc + 1], rhs=wt[b][:, kc, :],
                                                                   start=(kc == 0), stop=(kc == KC - 1)),
                             reads=[t_sc, t_wt[b]], writes=[t_pb[b]], inc=(kc == KC - 1))
                    S.op("dve", lambda e, b=b, n=n: e.tensor_tensor(out=brow[:, n * 512:(n + 1) * 512], in0=bank(b)[0:1, :],
                                                                     in1=brow[:, n * 512:(n + 1) * 512], op=ALU.add),
                         reads=[t_pb[b]], writes=[t_brow])
                S.op("dve", lambda e: e.scalar_tensor_tensor(out=brow[:, D:2 * D], in0=brow[:, D:2 * D], scalar=1.0,
                                                             in1=nrow[:, 0:D], op0=ALU.add, op1=ALU.mult),
                     reads=[t_nrow], writes=[t_brow])
                S.op("dve", lambda e: e.scalar_tensor_tensor(out=brow[:, 4 * D:5 * D], in0=brow[:, 4 * D:5 * D], scalar=1.0,
                                                             in1=nrow[:, D:2 * D], op0=ALU.add, op1=ALU.mult),
                     reads=[t_nrow], writes=[t_brow])
                S.dma(modscr[l:l + 1, :], brow, reads=[t_brow])
            tab = A.alloc([8], parts=64)
            oh = A.alloc([3 * 384], parts=64)
            ext = A.alloc([3 * 384], parts=8)
            t_tab = T_()
            t_oh = T_()
            t_ext = T_()
            S.op("dve", lambda e: e.memset(tab[32:64, :], NEG), writes=[t_tab])
            S.dma(tab[0:32, :], rel_bias[:, :], writes=[t_tab])
            S.dma(oh, onehot_in[:, :], writes=[t_oh])
            for p in range(3 if "o" not in SKIP else 0):
                S.op("pe", lambda e, p=p: e.matmul(bank(2 + p)[0:8, 0:384], lhsT=tab, rhs=oh[:, p * 384:(p + 1) * 384],
                                                   start=True, stop=True),
                     reads=[t_tab, t_oh], writes=[t_pb[0]])
                S.op("dve", lambda e, p=p: e.tensor_copy(out=ext[:, p * 384:(p + 1) * 384], in_=bank(2 + p)[0:8, 0:384]),
                     reads=[t_pb[0]], writes=[t_ext])
            S.dma(extscr[:, :], ext, reads=[t_ext])
            if os.environ.get("KDEBUG"):
                print("arena words used", A.off, "of", AW)
            S.barrier()
            S.release()
            Hk = [A.alloc([6, 128]) for _ in range(2)]
            Tm = [A.alloc([6, 128]) for _ in range(2)]
            MBt = [A.alloc([6, 128]) for _ in range(2)]
            t_H = [T_(), T_()]
            t_Tm = [T_(), T_()]
            t_MB = [T_(), T_()]
            for hd in range(8 if "h" not in SKIP else 0):
                b = hd % 2
                src = bass.AP(extscr.tensor, hd * 3 * 384, [[1, 128], [128, 6], [1, 128]])
                for p in range(3):
                    srcp = bass.AP(extscr.tensor, hd * 3 * 384 + p * 384, [[1, 128], [128, 2], [1, 128]])
                    S.dma(Hk[b][:, 2 * p:2 * p + 2, :], srcp, writes=[t_H[b]])
                S.op("dve", lambda e, b=b: e.tensor_copy(out=Tm[b], in_=Hk[b][:, :, ::-1]), reads=[t_H[b]], writes=[t_Tm[b]])
                S.op("act", lambda e, b=b: e.activation(out=MBt[b], in_=Tm[b], func=AF.Exp), reads=[t_Tm[b]], writes=[t_MB[b]])
                S.dma(mbscr[hd], MBt[b].rearrange("p a b -> p (a b)"), reads=[t_MB[b]])
            if os.environ.get("KDEBUG"):
                print("arena words used", A.off, "of", AW)
            S.barrier()
            S.release()

        def load_mod_cols(l, idx_gm, idx_sh):
            gmT = A.alloc([16])
            shT = A.alloc([16])
            t_g = T_()
            S.dma(gmT, modscr[l, idx_gm * D:(idx_gm + 1) * D].rearrange("(kc p) -> p kc", p=128), writes=[t_g],
                  allow_slow_non_contiguous=True)
            S.dma(shT, modscr[l, idx_sh * D:(idx_sh + 1) * D].rearrange("(kc p) -> p kc", p=128), writes=[t_g],
                  allow_slow_non_contiguous=True)
            return gmT, shT, t_g

        def bcast_row(dst, src_tensor_ap, offset, t_dst):
            S.dma(dst, bass.AP(src_tensor_ap.tensor, offset, [[0, 128], [1, D]]), writes=[t_dst])

        class NormCtx:
            pass

        def make_norm_ctx():
            n = NormCtx()
            n.sq = A.alloc([D], BF16)
            n.t_sq = T_()
            n.ss = [A.alloc([1]) for _ in range(4)]
            n.rs = [A.alloc([1]) for _ in range(4)]
            n.t_ss = [T_() for _ in range(4)]
            n.t_rs = [T_() for _ in range(4)]
            n.xn = [A.alloc([D], BF16) for _ in range(2)]
            n.t_xn = [T_(), T_()]
            n.t_tp = [T_() for _ in range(4)]
            n.cnt = 0
            return n

        def tp_region(r):
            base = r * 512
            return ps[:, base:base + 256].bitcast(BF16).rearrange("p (a b) -> p a b", a=4)

        def rstd_of(n, src, t_src, k):
            S.op("act", lambda e: e.activation(out=n.sq, in_=src, func=AF.Square, scale=1.0 / math.sqrt(D), accum_out=n.ss[k]),
                 reads=[t_src], writes=[n.t_sq, n.t_ss[k]])
            S.op("act", lambda e: e.activation(out=n.rs[k], in_=n.ss[k], func=AF.Sqrt, bias=EPS, scale=1.0),
                 reads=[n.t_ss[k]], writes=[n.t_rs[k]])
            S.op("dve", lambda e: e.reciprocal(out=n.rs[k], in_=n.rs[k]), reads=[n.t_rs[k]], writes=[n.t_rs[k]])

        def norm_mod_transpose(n, src, t_src, gmT, shT, t_g, dst_fn, t_dst_fn, defer=False):
            k = n.cnt % 4
            b = n.cnt % 2
            n.cnt += 1
            rstd_of(n, src, t_src, k)
            S.op("dve", lambda e: e.tensor_scalar(out=n.xn[b], in0=src, scalar1=n.rs[k], scalar2=None, op0=ALU.mult),
                 reads=[t_src, n.t_rs[k]], writes=[n.t_xn[b]])

            def part2():
                _nmt_part2(n, b, gmT, shT, t_g, dst_fn, t_dst_fn)
            if defer:
                return part2
            part2()
            return None

        def _nmt_part2(n, b, gmT, shT, t_g, dst_fn, t_dst_fn):
            for g4 in range(4):
                r = g4
                reg = tp_region(r)
                for j in range(4):
                    kc = g4 * 4 + j
                    S.op("pe", lambda e, reg=reg, j=j, kc=kc, b=b: e.transpose(reg[:, j, :], n.xn[b][:, kc * 128:(kc + 1) * 128], ident),
                         reads=[n.t_xn[b], t_ident], writes=[n.t_tp[r]], inc=(j == 3))
                for j in range(4):
                    kc = g4 * 4 + j
                    if g4 % 2 == 0:
                        S.op("act", lambda e, reg=reg, j=j, kc=kc: e.activation(out=dst_fn(kc), in_=reg[:, j, :], func=AF.Identity,
                                                                                 scale=gmT[:, kc:kc + 1], bias=shT[:, kc:kc + 1]),
                             reads=[n.t_tp[r], t_g], writes=[t_dst_fn(kc)])
                    else:
                        S.op("dve", lambda e, reg=reg, j=j, kc=kc: e.tensor_scalar(out=dst_fn(kc), in0=reg[:, j, :], scalar1=gmT[:, kc:kc + 1],
                                                                                   scalar2=shT[:, kc:kc + 1], op0=ALU.mult, op1=ALU.add),
                             reads=[n.t_tp[r], t_g], writes=[t_dst_fn(kc)])

        def phase_a(l, xsrc):
            A.reset(A_base)
            TB = 1024 if T >= 1024 else T
            NTI = TB // 128
            NJ = TB // 512
            gmT, shT, t_g = load_mod_cols(l, 1, 0)
            n = make_norm_ctx()
            xt = [A.alloc([D]) for _ in range(2)]
            t_xt = [T_(), T_()]
            hT2 = [A.alloc([KC, TB], BF16) for _ in range(2)]
            t_h2b = [[[T_() for _ in range(NTI)] for _ in range(KC)] for _ in range(2)]
            wb = [A.alloc([KC, 512], BF16) for _ in range(2)]
            t_wb = [T_(), T_()]
            ob = [A.alloc([512]) for _ in range(4)]
            t_ob = [T_() for _ in range(4)]
            t_bk = [T_() for _ in range(4)]
            gcnt = [0]
            scale_q = 1.0 / math.sqrt(128.0)
            wcnt = [0]

            def load_w(cb):
                b = wcnt[0] % 2
                wcnt[0] += 1
                S.dma(wb[b], win_bf[l, cb], writes=[t_wb[b]], reads=[t_wc[("win", l)]])
                return b

            xcnt = [0]

            def norm_tile(tbn, i, defer):
                tok0 = tbn * TB + i * 128
                b = xcnt[0] % 2
                xcnt[0] += 1
                hbuf = hT2[tbn % 2]
                thb = t_h2b[tbn % 2]
                S.dma(xt[b], xsrc[tok0:tok0 + 128, :], writes=[t_xt[b]])
                return norm_mod_transpose(n, xt[b], t_xt[b], gmT, shT, t_g,
                                          lambda kc, i=i, hbuf=hbuf: hbuf[:, kc, i * 128:(i + 1) * 128], lambda kc, i=i, thb=thb: thb[kc][i],
                                          defer=defer)

            NBLK = T // TB
            for i in range(NTI):
                norm_tile(0, i, False)
            for tb in range(NBLK):
                nxt_w = load_w(0)
                hT = hT2[tb % 2]
                t_h = t_h2b[tb % 2]
                pend = None
                cbs = list(range(10))
                for cb in cbs:
                    if pend is not None:
                        pend()
                        pend = None
                    if tb + 1 < NBLK and cb < NTI:
                        pend = norm_tile(tb + 1, cb, True)
                    wbi = nxt_w
                    if cb + 1 < 10:
                        nxt_w = load_w(cb + 1)
                    W = wb[wbi]
                    if cb in (4, 5):
                        for i in range(NTI):
                            tok0 = tb * TB + i * 128
                            g = gcnt[0]
                            gcnt[0] += 1
                            bk = 4 + g % 4
                            for kc in range(KC):
                                S.op("pe", lambda e, bk=bk, kc=kc, i=i, W=W, hT=hT: e.matmul(bank(bk), lhsT=hT[:, kc, i * 128:(i + 1) * 128], rhs=W[:, kc, :],
                                                                                       start=(kc == 0), stop=(kc == KC - 1)),
                                     reads=[t_h[kc][i], t_wb[wbi]], writes=[t_bk[bk - 4]], inc=(kc == KC - 1))
                            o = g % 4
                            obv = ob[o].bitcast(BF16)[:, 0:512]
                            eng = "act" if bk % 2 == 0 else "dve"
                            if eng == "act":
                                S.op("act", lambda e, bk=bk, obv=obv: e.activation(out=obv, in_=bank(bk), func=AF.Copy),
                                     reads=[t_bk[bk - 4]], writes=[t_ob[o]])
                            else:
                                S.op("dve", lambda e, bk=bk, obv=obv: e.tensor_copy(out=obv, in_=bank(bk)),
                                     reads=[t_bk[bk - 4]], writes=[t_ob[o]])
                            S.dma(vv[tok0:tok0 + 128, (cb - 4) * 512:(cb - 3) * 512], obv, reads=[t_ob[o]])
                    else:
                        for m in range(4):
                            f = cb * 512 + m * 128
                            kind = f // 1024
                            hd = (f % 1024) // 128
                            for j in range(NJ):
                                g = gcnt[0]
                                gcnt[0] += 1
                                bk = 4 + g % 4
                                for kc in range(KC):
                                    S.op("pe", lambda e, bk=bk, kc=kc, m=m, j=j, W=W, hT=hT: e.matmul(bank(bk), lhsT=W[:, kc, m * 128:(m + 1) * 128],
                                                                                                rhs=hT[:, kc, j * 512:(j + 1) * 512],
                                                                                                start=(kc == 0), stop=(kc == KC - 1)),
                                         reads=[t_h[kc][j * 4 + q4] for q4 in range(4)] + [t_wb[wbi]], writes=[t_bk[bk - 4]], inc=(kc == KC - 1))
                                o = g % 4
                                tk0 = tb * TB + j * 512
                                if kind <= 1:
                                    obv = ob[o].bitcast(BF16)[:, 0:512]
                                    dst = (qT if kind == 0 else kT)[hd][:, tk0:tk0 + 512]
                                    sc = scale_q if kind == 0 else 1.0
                                else:
                                    obv = ob[o]
                                    dst = (xrT if kind == 3 else grT)[hd][:, tk0:tk0 + 512]
                                    sc = 1.0
                                if bk % 2 == 0:
                                    S.op("act", lambda e, bk=bk, obv=obv, sc=sc: e.activation(out=obv, in_=bank(bk), func=AF.Copy, scale=sc),
                                         reads=[t_bk[bk - 4]], writes=[t_ob[o]])
                                else:
                                    S.op("dve", lambda e, bk=bk, obv=obv, sc=sc: e.tensor_scalar(out=obv, in0=bank(bk), scalar1=sc, scalar2=None, op0=ALU.mult),
                                         reads=[t_bk[bk - 4]], writes=[t_ob[o]])
                                S.dma(dst, obv, reads=[t_ob[o]])
            if os.environ.get("KDEBUG"):
                print("arena words used", A.off, "of", AW)
            S.barrier()
            S.release()

        def phase_attn(l):
            A.reset(A_base)
            NCH = T // 128
            qs2 = [A.alloc([T], BF16) for _ in range(2)]
            ks2 = [A.alloc([T], BF16) for _ in range(2)]
            t_q2 = [T_(), T_()]
            t_k2 = [T_(), T_()]
            Eh2 = [A.alloc([6, 128]) for _ in range(2)]
            t_E2 = [T_(), T_()]
            Vb = [A.alloc([NCH, 128], BF16) for _ in range(2)]
            t_V = [T_(), T_()]
            num = A.alloc([T])
            den = A.alloc([T])
            t_num = T_()
            t_den = T_()
            ao = A.alloc([T], BF16)
            t_ao = T_()
            PT = [A.alloc([2, 128], BF16) for _ in range(4)]
            t_PT = [T_() for _ in range(4)]
            XS = [A.alloc([2, 128]) for _ in range(4)]
            t_XS = [T_() for _ in range(4)]
            t_S = [T_() for _ in range(4)]
            t_O = [T_(), T_()]
            t_Dn = [T_(), T_()]

            def load_head(hd):
                hb = hd % 2
                S.dma(qs2[hb], qT[hd], writes=[t_q2[hb]])
                S.dma(ks2[hb], kT[hd], writes=[t_k2[hb]])
                S.dma(Eh2[hb].rearrange("p a b -> p (a b)"), mbscr[hd], writes=[t_E2[hb]])

            vlist = [(hd, p) for hd in range(8) for p in range(3)]

            def load_v(vi):
                hd, p = vlist[vi]
                dil = PATTERNS[p][1]
                nchr = (T // dil) // 128
                vsrc = bass.AP(vv.tensor, hd * 128, [[dil * 1024, 128], [1024, dil], [128 * dil * 1024, nchr], [1, 128]])
                S.dma(Vb[vi % 2].rearrange("p (r c) d -> p r c d", r=dil), vsrc, writes=[t_V[vi % 2]])

            def Sreg(r):
                base = (0, 1, 6, 7)[r] * 512
                return ps[:, base:base + 256].rearrange("p (a b) -> p a b", a=2)

            tcnt = [0]
            gcnt = [0]

            def sl(j0, n, r, dil):
                s0 = r + dil * j0
                return slice(s0, s0 + dil * (n - 1) + 1, dil)

            def do_tile(hb, p, r, dil, vb, jq0, nq, chunks, mb0, mbcol0, ocol, gb):
                qs, ks, Eh = qs2[hb], ks2[hb], Eh2[hb]
                t_q, t_k, t_E = t_q2[hb], t_k2[hb], t_E2[hb]
                L = T // dil
                nchr = L // 128
                ti = tcnt[0] % 4
                tcnt[0] += 1
                Sr = Sreg(ti)
                qsl = sl(jq0, nq, r, dil)
                ncu = len(chunks)
                for ci, kc_ in enumerate(chunks):
                    ksl = sl(kc_ * 128, 128, r, dil)
                    S.op("pe", lambda e, Sr=Sr, ci=ci, ksl=ksl, qsl=qsl, nq=nq, ks=ks, qs=qs: e.matmul(Sr[:, ci, 0:nq], lhsT=ks[:, ksl], rhs=qs[:, qsl], start=True, stop=True),
                         reads=[t_k, t_q], writes=[t_S[ti]], inc=(ci == ncu - 1))
                S.op("act", lambda e, Sr=Sr, ti=ti, ncu=ncu, nq=nq: e.activation(out=XS[ti][:, 0:ncu, 0:nq], in_=Sr[:, 0:ncu, 0:nq], func=AF.Exp),
                     reads=[t_S[ti]], writes=[t_XS[ti]])
                e0 = p * 2 + mb0
                S.op("pool", lambda e, ti=ti, ncu=ncu, nq=nq, e0=e0, Eh=Eh: e.tensor_tensor(out=PT[ti][:, 0:ncu, 0:nq], in0=XS[ti][:, 0:ncu, 0:nq],
                                                                                         in1=Eh[:, e0:e0 + ncu, mbcol0:mbcol0 + nq], op=ALU.mult),
                     reads=[t_XS[ti], t_E], writes=[t_PT[ti]])
                ob_ = bank(2 + gb)
                db_ = bank(4 + gb)

                def partB():
                    for ci, kc_ in enumerate(chunks):
                        vch = r * nchr + kc_
                        S.op("pe", lambda e, ci=ci, vch=vch: e.matmul(ob_[:, ocol:ocol + nq], lhsT=Vb[vb][:, vch, :], rhs=PT[ti][:, ci, 0:nq],
                                                                       start=(ci == 0), stop=(ci == ncu - 1)),
                             reads=[t_V[vb], t_PT[ti]], writes=[t_O[gb]], inc=False)
                    for ci, kc_ in enumerate(chunks):
                        S.op("pe", lambda e, ci=ci: e.matmul(db_[:, ocol:ocol + nq], lhsT=ones, rhs=PT[ti][:, ci, 0:nq],
                                                              start=(ci == 0), stop=(ci == ncu - 1)),
                             reads=[t_ones, t_PT[ti]], writes=[t_Dn[gb]], inc=(ci == ncu - 1))
                fifo.append(("B", partB))
                drain(3)

            fifo = []

            def drain(keep):
                while sum(1 for k_, _ in fifo if k_ == "B") > keep:
                    k_, f_ = fifo.pop(0)
                    f_()
                    while fifo and fifo[0][0] == "F":
                        fifo.pop(0)[1]()

            def flush(gb, ocol0, nq, jq0, r, dil, first):
                fifo.append(("F", lambda: flush_now(gb, ocol0, nq, jq0, r, dil, first)))

            def flush_now(gb, ocol0, nq, jq0, r, dil, first):
                csl = sl(jq0, nq, r, dil)
                ob_ = bank(2 + gb)
                db_ = bank(4 + gb)
                if first:
                    S.op("dve", lambda e: e.tensor_copy(out=num[:, csl], in_=ob_[:, ocol0:ocol0 + nq]), reads=[t_O[gb]], writes=[t_num])
                    S.op("dve", lambda e: e.tensor_copy(out=den[:, csl], in_=db_[:, ocol0:ocol0 + nq]), reads=[t_Dn[gb]], writes=[t_den])
                else:
                    S.op("dve", lambda e: e.tensor_tensor(out=num[:, csl], in0=ob_[:, ocol0:ocol0 + nq], in1=num[:, csl], op=ALU.add),
                         reads=[t_O[gb]], writes=[t_num])
                    S.op("dve", lambda e: e.tensor_tensor(out=den[:, csl], in0=db_[:, ocol0:ocol0 + nq], in1=den[:, csl], op=ALU.add),
                         reads=[t_Dn[gb]], writes=[t_den])

            load_head(0)
            load_v(0)
            nheads = 8 if os.environ.get("KSTOP", "") not in ("attn1", "rnn1") else 1
            for hd in range(nheads):
                hb = hd % 2
                if hd + 1 < nheads:
                    load_head(hd + 1)
                for p, (win, dil) in enumerate(PATTERNS):
                    vi = hd * 3 + p
                    drain(0)
                    while fifo:
                        fifo.pop(0)[1]()
                    if vi + 1 < nheads * 3:
                        load_v(vi + 1)
                    L = T // dil
                    nchr = L // 128
                    vb = vi % 2
                    first = (p == 0)
                    for r in range(dil):
                        c = 0
                        while c < nchr - 1:
                            gsz = min(4, nchr - 1 - c)
                            gb = gcnt[0] % 2
                            gcnt[0] += 1
                            for gi in range(gsz):
                                cc = c + gi
                                do_tile(hb, p, r, dil, vb, 64 + 128 * cc, 128, [cc, cc + 1], 0, 0, gi * 128, gb)
                            flush(gb, 0, gsz * 128, 64 + 128 * c, r, dil, first)
                            c += gsz
                        gb = gcnt[0] % 2
                        gcnt[0] += 1
                        do_tile(hb, p, r, dil, vb, 0, 64, [0], 1, 64, 0, gb)
                        do_tile(hb, p, r, dil, vb, L - 64, 64, [nchr - 1], 0, 0, 64, gb)
                        flush(gb, 0, 64, 0, r, dil, first)
                        flush(gb, 64, 64, L - 64, r, dil, first)
                drain(0)
                while fifo:
                    fifo.pop(0)[1]()
                S.op("dve", lambda e: e.reciprocal(out=den, in_=den), reads=[t_den], writes=[t_den])
                S.op("dve", lambda e: e.tensor_tensor(out=ao, in0=num, in1=den, op=ALU.mult), reads=[t_num, t_den], writes=[t_ao])
                S.dma(mixT[hd], ao, reads=[t_ao])
            if os.environ.get("KDEBUG"):
                print("arena words used", A.off, "of", AW)
            S.barrier()
            S.release()

        def phase_rnn(l, hd):
            A.reset(A_base)
            CH = min(2048, T)
            NCk = T // CH
            u = A.alloc([T])
            ubf = A.alloc([T], BF16)
            yacc = A.alloc([T])
            t_u = [T_() for _ in range(NCk)]
            t_ub = [T_() for _ in range(NCk)]
            t_y = [T_() for _ in range(NCk)]
            xp = A.alloc([CH + 3])
            t_xp = T_()
            rt2 = [A.alloc([CH]) for _ in range(2)]
            st2 = [A.alloc([CH]) for _ in range(2)]
            it2 = [A.alloc([CH]) for _ in range(2)]
            hs2 = [A.alloc([CH]) for _ in range(2)]
            gr2 = [A.alloc([CH]) for _ in range(2)]
            t_rt2, t_st2, t_it2, t_hs2, t_gr2 = ([T_(), T_()] for _ in range(5))
            obf2 = [A.alloc([CH], BF16) for _ in range(2)]
            t_obf2 = [T_(), T_()]
            cctr = [0]
            wg = A.alloc([4, 128], BF16)
            t_wg = T_()
            bg = A.alloc([4])
            cw = A.alloc([5])
            lm = A.alloc([2])
            cl = A.alloc([2])
            t_sm = T_()
            t_cl = T_()
            carry = A.alloc([1])
            t_carry = T_()
            hsl = slice(hd * 128, (hd + 1) * 128)
            for d in range(2):
                S.dma(wg[:, d, :], wrg_bf[l, d, hd], writes=[t_wg], reads=[t_wc[("wrg", l)]])
                S.dma(wg[:, 2 + d, :], wig_bf[l, d, hd], writes=[t_wg], reads=[t_wc[("wrg", l)]])
                S.dma(bg[:, d:d + 1], b_rg[l, d, hsl].rearrange("(p o) -> p o", o=1), writes=[t_sm])
                S.dma(bg[:, 2 + d:3 + d], b_ig[l, d, hsl].rearrange("(p o) -> p o", o=1), writes=[t_sm])
                S.dma(lm[:, d:d + 1], lam[l, d, hsl].rearrange("(p o) -> p o", o=1), writes=[t_sm])
            for j in range(4):
                S.dma(cw[:, j:j + 1], rnn_conv_w[l, j, hsl].rearrange("(p o) -> p o", o=1), writes=[t_sm])
            S.dma(cw[:, 4:5], rnn_conv_b[l, hsl].rearrange("(p o) -> p o", o=1), writes=[t_sm])
            S.op("act", lambda e: e.activation(out=cl, in_=lm, func=AF.Exp, scale=-1.0), reads=[t_sm], writes=[t_cl])
            S.op("act", lambda e: e.activation(out=cl, in_=cl, func=AF.Ln, bias=1.0, scale=1.0), reads=[t_cl], writes=[t_cl])
            S.op("dve", lambda e: e.tensor_scalar(out=cl, in0=cl, scalar1=-8.0, scalar2=None, op0=ALU.mult), reads=[t_cl], writes=[t_cl])
            for ck in range(NCk):
                c0 = ck * CH
                lo = max(c0 - 2, 0)
                hi = min(c0 + CH + 1, T)
                if ck == 0:
                    S.op("dve", lambda e: e.memset(xp[:, 0:2], 0.0), writes=[t_xp])
                if ck == NCk - 1:
                    S.op("dve", lambda e: e.memset(xp[:, CH + 2:CH + 3], 0.0), writes=[t_xp])
                S.dma(xp[:, lo - (c0 - 2):hi - (c0 - 2)], xrT[hd][:, lo:hi], writes=[t_xp])
                uc = u[:, c0:c0 + CH]
                S.op("dve", lambda e, uc=uc: e.tensor_scalar(out=uc, in0=xp[:, 0:CH], scalar1=cw[:, 0:1], scalar2=cw[:, 4:5], op0=ALU.mult, op1=ALU.add),
                     reads=[t_xp, t_sm], writes=[t_u[ck]])
                for j in range(1, 4):
                    S.op("dve", lambda e, uc=uc, j=j: e.scalar_tensor_tensor(out=uc, in0=xp[:, j:j + CH], scalar=cw[:, j:j + 1], in1=uc, op0=ALU.mult, op1=ALU.add),
                         reads=[t_xp, t_sm, t_u[ck]], writes=[t_u[ck]])
                S.op("act", lambda e, uc=uc, c0=c0: e.activation(out=ubf[:, c0:c0 + CH], in_=uc, func=AF.Copy), reads=[t_u[ck]], writes=[t_ub[ck]])
            t_gb = [T_() for _ in range(8)]
            for d in range(2):
                order = range(NCk) if d == 0 else range(NCk - 1, -1, -1)
                for oi, ck in enumerate(order):
                    c0 = ck * CH
                    nb4 = CH // 512
                    pb = cctr[0] % 2
                    cctr[0] += 1
                    rt, s_t, it, hs, grt, obf = rt2[pb], st2[pb], it2[pb], hs2[pb], gr2[pb], obf2[pb]
                    t_rt, t_st, t_it, t_hs, t_gr, t_obf = t_rt2[pb], t_st2[pb], t_it2[pb], t_hs2[pb], t_gr2[pb], t_obf2[pb]
                    if d == 1:
                        S.dma(grt, grT[hd][:, c0:c0 + CH], writes=[t_gr])
                        S.op("act", lambda e, grt=grt: e.activation(out=grt, in_=grt, func=AF.Gelu_apprx_tanh), reads=[t_gr], writes=[t_gr])
                    for gi, (wi, dst, t_dst) in enumerate(((d, rt, t_rt), (2 + d, it, t_it))):
                        for q4 in range(nb4):
                            bk = gi * 4 + q4
                            S.op("pe", lambda e, rt=rt, s_t=s_t, it=it, hs=hs, grt=grt, obf=obf, bk=bk, wi=wi, q4=q4, c0=c0: e.matmul(bank(bk), lhsT=wg[:, wi, :], rhs=ubf[:, c0 + q4 * 512:c0 + (q4 + 1) * 512],
                                                                                       start=True, stop=True),
                                 reads=[t_wg, t_ub[ck]], writes=[t_gb[bk]])
                            S.op("act", lambda e, rt=rt, s_t=s_t, it=it, hs=hs, grt=grt, obf=obf, bk=bk, wi=wi, q4=q4, dst=dst: e.activation(out=dst[:, q4 * 512:(q4 + 1) * 512], in_=bank(bk), func=AF.Sigmoid,
                                                                                              bias=bg[:, wi:wi + 1], scale=1.0),
                                 reads=[t_gb[bk], t_sm], writes=[t_dst])
                    S.op("act", lambda e, rt=rt, s_t=s_t, it=it, hs=hs, grt=grt, obf=obf, d=d: e.activation(out=rt, in_=rt, func=AF.Exp, scale=cl[:, d:d + 1]), reads=[t_rt, t_cl], writes=[t_rt])
                    S.op("dve", lambda e, rt=rt, s_t=s_t, it=it, hs=hs, grt=grt, obf=obf: e.tensor_tensor(out=s_t, in0=rt, in1=rt, op=ALU.mult), reads=[t_rt], writes=[t_st])
                    S.op("dve", lambda e, rt=rt, s_t=s_t, it=it, hs=hs, grt=grt, obf=obf: e.tensor_scalar(out=s_t, in0=s_t, scalar1=-1.0, scalar2=1.0, op0=ALU.mult, op1=ALU.add), reads=[t_st], writes=[t_st])
                    S.op("act", lambda e, rt=rt, s_t=s_t, it=it, hs=hs, grt=grt, obf=obf: e.activation(out=s_t, in_=s_t, func=AF.Sqrt), reads=[t_st], writes=[t_st])
                    S.op("dve", lambda e, rt=rt, s_t=s_t, it=it, hs=hs, grt=grt, obf=obf, c0=c0: e.tensor_tensor(out=it, in0=it, in1=u[:, c0:c0 + CH], op=ALU.mult), reads=[t_it, t_u[ck]], writes=[t_it])
                    S.op("dve", lambda e, rt=rt, s_t=s_t, it=it, hs=hs, grt=grt, obf=obf: e.tensor_tensor(out=it, in0=it, in1=s_t, op=ALU.mult), reads=[t_it, t_st], writes=[t_it])
                    init = 0.0 if oi == 0 else carry
                    if d == 0:
                        yc = yacc[:, c0:c0 + CH]
                        S.op("dve", lambda e, rt=rt, s_t=s_t, it=it, hs=hs, grt=grt, obf=obf, yc=yc, init=init: e.tensor_tensor_scan(out=yc, data0=rt, data1=it, initial=init, op0=ALU.mult, op1=ALU.add),
                             reads=[t_rt, t_it, t_carry], writes=[t_y[ck]])
                        S.op("dve", lambda e, rt=rt, s_t=s_t, it=it, hs=hs, grt=grt, obf=obf, yc=yc: e.tensor_copy(out=carry, in_=yc[:, CH - 1:CH]), reads=[t_y[ck]], writes=[t_carry])
                    else:
                        yc = yacc[:, c0:c0 + CH]
                        S.op("dve", lambda e, rt=rt, s_t=s_t, it=it, hs=hs, grt=grt, obf=obf, init=init: e.tensor_tensor_scan(out=hs[:, ::-1], data0=rt[:, ::-1], data1=it[:, ::-1], initial=init,
                                                                              op0=ALU.mult, op1=ALU.add),
                             reads=[t_rt, t_it, t_carry], writes=[t_hs])
                        S.op("dve", lambda e, rt=rt, s_t=s_t, it=it, hs=hs, grt=grt, obf=obf: e.tensor_copy(out=carry, in_=hs[:, 0:1]), reads=[t_hs], writes=[t_carry])
                        S.op("dve", lambda e, rt=rt, s_t=s_t, it=it, hs=hs, grt=grt, obf=obf, yc=yc: e.tensor_tensor(out=yc, in0=yc, in1=hs, op=ALU.add), reads=[t_hs, t_y[ck]], writes=[t_y[ck]])
                        S.op("dve", lambda e, rt=rt, s_t=s_t, it=it, hs=hs, grt=grt, obf=obf, yc=yc: e.tensor_tensor(out=obf, in0=yc, in1=grt, op=ALU.mult), reads=[t_gr, t_y[ck]], writes=[t_obf])
                        S.dma(mixT[8 + hd][:, c0:c0 + CH], obf, reads=[t_obf])
            if os.environ.get("KDEBUG"):
                print("arena words used", A.off, "of", AW)
            S.barrier()
            S.release()

        def phase_ca(l, xsrc):
            A.reset(A_base)
            TB = 512
            gmT, shT, t_g = load_mod_cols(l, 4, 3)
            n = make_norm_ctx()
            wo = A.alloc([KC, D], BF16)
            t_wo = T_()
            for n4 in range(4):
                S.dma(wo[:, :, n4 * 512:(n4 + 1) * 512], wout_bf[l][:, :, n4 * 512:(n4 + 1) * 512], writes=[t_wo], reads=[t_wc[("wout", l)]])
            g1 = A.alloc([D])
            t_g1 = T_()
            bcast_row(g1, modscr, l * 6 * D + 2 * D, t_g1)
            mx = [A.alloc([KC, TB], BF16) for _ in range(2)]
            t_mx = [T_(), T_()]
            xt = [A.alloc([D]) for _ in range(2)]
            t_xt = [T_(), T_()]
            x1 = [A.alloc([D]) for _ in range(2)]
            t_x1 = [T_(), T_()]
            h2 = [A.alloc([KC, TB], BF16) for _ in range(2)]
            t_h2 = [[[T_() for _ in range(4)] for _ in range(KC)] for _ in range(2)]
            t_bk = [T_() for _ in range(4)]
            NB = T // TB
            S.dma(mx[0], mixT[:, :, 0:TB].rearrange("k p t -> p k t"), writes=[t_mx[0]])
            cnt = 0
            pend = [None]
            pstore = [None]
            for tb in range(NB):
                mb = tb % 2
                if tb + 1 < NB:
                    S.dma(mx[1 - mb], mixT[:, :, (tb + 1) * TB:(tb + 2) * TB].rearrange("k p t -> p k t"), writes=[t_mx[1 - mb]])
                for i in range(4):
                    tok0 = tb * TB + i * 128
                    b = cnt % 2
                    cnt += 1
                    S.dma(xt[b], xsrc[tok0:tok0 + 128, :], writes=[t_xt[b]])
                    for n4 in range(4):
                        bk = 4 + n4
                        for kc in range(KC):
                            S.op("pe", lambda e, bk=bk, kc=kc, i=i, mb=mb, n4=n4: e.matmul(bank(bk), lhsT=mx[mb][:, kc, i * 128:(i + 1) * 128],
                                                                                            rhs=wo[:, kc, n4 * 512:(n4 + 1) * 512],
                                                                                            start=(kc == 0), stop=(kc == KC - 1)),
                                 reads=[t_mx[mb], t_wo], writes=[t_bk[n4]], inc=(kc == KC - 1))
                        S.op("dve", lambda e, bk=bk, n4=n4, b=b: e.tensor_tensor(out=x1[b][:, n4 * 512:(n4 + 1) * 512], in0=bank(bk),
                                                                                  in1=g1[:, n4 * 512:(n4 + 1) * 512], op=ALU.mult),
                             reads=[t_bk[n4], t_g1], writes=[t_x1[b]])
                    if pend[0] is not None:
                        pend[0]()
                        pend[0] = None
                        if pstore[0] is not None:
                            pstore[0]()
                            pstore[0] = None
                    S.op("dve", lambda e, b=b: e.tensor_tensor(out=x1[b], in0=x1[b], in1=xt[b], op=ALU.add), reads=[t_x1[b], t_xt[b]], writes=[t_x1[b]])
                    S.dma(x1s[tok0:tok0 + 128, :], x1[b], reads=[t_x1[b]])
                    pend[0] = norm_mod_transpose(n, x1[b], t_x1[b], gmT, shT, t_g,
                                                 lambda kc, i=i, mb=mb: h2[mb][:, kc, i * 128:(i + 1) * 128], lambda kc, i=i, mb=mb: t_h2[mb][kc][i],
                                                 defer=True)
                pstore[0] = (lambda tb=tb, mb=mb: S.dma(h2T[:, :, tb * TB:(tb + 1) * TB].rearrange("k p t -> p k t"), h2[mb],
                                                        reads=[t_h2[mb][kc][i] for kc in range(KC) for i in range(4)]))
            if pend[0] is not None:
                pend[0]()
                pend[0] = None
            if pstore[0] is not None:
                pstore[0]()
                pstore[0] = None
            if os.environ.get("KDEBUG"):
                print("arena words used", A.off, "of", AW)
            S.barrier()
            S.release()

        def phase_cb(l, last):
            A.reset(A_base)
            TB = 512
            NB = T // TB
            g2 = A.alloc([D])
            t_g2 = T_()
            bcast_row(g2, modscr, l * 6 * D + 5 * D, t_g2)
            if last:
                fn = A.alloc([D])
                t_fn = T_()
                bcast_row(fn, final_norm, 0, t_fn)
                n = make_norm_ctx()
            fw = A.alloc([4, 48])
            t_fw = T_()
            for j in range(3):
                S.dma(fw[:, j, :], ffn_conv_w[l, j, :].rearrange("(mc p) -> p mc", p=128), writes=[t_fw], allow_slow_non_contiguous=True)
            S.dma(fw[:, 3, :], ffn_conv_b[l, :].rearrange("(mc p) -> p mc", p=128), writes=[t_fw], allow_slow_non_contiguous=True)
            hb = A.alloc([KC, TB], BF16)
            t_hb = T_()
            halo = A.alloc([KC, 2], BF16)
            t_halo = T_()
            wgv = [A.alloc([2, KC, 128], BF16) for _ in range(2)]
            t_wgv = [T_(), T_()]
            actT = A.alloc([48, TB], BF16)
            t_act = [T_() for _ in range(48)]
            gsb = [A.alloc([TB + 2]) for _ in range(2)]
            t_gsb = [T_(), T_()]
            cac = [A.alloc([TB]) for _ in range(2)]
            t_cac = [T_(), T_()]
            sg = [A.alloc([TB]) for _ in range(2)]
            t_sg = [T_(), T_()]
            vsb = [A.alloc([TB]) for _ in range(2)]
            t_vsb = [T_(), T_()]
            wd = [A.alloc([16, 512], BF16) for _ in range(2)]
            t_wd = [T_(), T_()]
            xr_ = [A.alloc([D]) for _ in range(4)]
            t_xr = [T_() for _ in range(4)]
            t_G = [T_(), T_()]
            t_Vp = T_()
            t_Hh = [T_(), T_()]
            t_acc = [T_() for _ in range(4)]
            tmp = sg
            t_tmp = t_sg
            wcnt = 0
            dcnt = 0
            ecnt = 0
            for tb in range(NB):
                tok0 = tb * TB
                S.dma(hb, h2T[:, :, tok0:tok0 + TB].rearrange("k p t -> p k t"), writes=[t_hb])
                if tb == 0:
                    S.op("dve", lambda e: e.memset(halo[:, :, 0:1], 0.0), writes=[t_halo])
                else:
                    S.dma(halo[:, :, 0:1], h2T[:, :, tok0 - 1:tok0].rearrange("k p t -> p k t"), writes=[t_halo], allow_slow_non_contiguous=True)
                if tb == NB - 1:
                    S.op("dve", lambda e: e.memset(halo[:, :, 1:2], 0.0), writes=[t_halo])
                else:
                    S.dma(halo[:, :, 1:2], h2T[:, :, tok0 + TB:tok0 + TB + 1].rearrange("k p t -> p k t"), writes=[t_halo], allow_slow_non_contiguous=True)
                for i in range(4):
                    S.dma(xr_[i], x1s[tok0 + i * 128:tok0 + (i + 1) * 128, :], writes=[t_xr[i]])
                for mc in range(48):
                    wbi = wcnt % 2
                    wcnt += 1
                    S.dma(wgv[wbi], wup_bf[l, mc].rearrange("g p k n -> p g k n"), writes=[t_wgv[wbi]], reads=[t_wc[("wup", l)]])
                    gb = mc % 2
                    Gb = bank(4 + gb)
                    Vp = bank(6)
                    Hh = bank(7)[:, 0:2]
                    for kc in range(KC):
                        S.op("pe", lambda e, Gb=Gb, kc=kc, wbi=wbi: e.matmul(Gb, lhsT=wgv[wbi][:, 0, kc, :], rhs=hb[:, kc, :], start=(kc == 0), stop=(kc == KC - 1)),
                             reads=[t_wgv[wbi], t_hb], writes=[t_G[gb]], inc=(kc == KC - 1))
                    for kc in range(KC):
                        S.op("pe", lambda e, Hh=Hh, kc=kc, wbi=wbi: e.matmul(Hh, lhsT=wgv[wbi][:, 0, kc, :], rhs=halo[:, kc, :], start=(kc == 0), stop=(kc == KC - 1)),
                             reads=[t_wgv[wbi], t_halo], writes=[t_Hh[0]], inc=(kc == KC - 1))
                    for kc in range(KC):
                        S.op("pe", lambda e, Vp=Vp, kc=kc, wbi=wbi: e.matmul(Vp, lhsT=wgv[wbi][:, 1, kc, :], rhs=hb[:, kc, :], start=(kc == 0), stop=(kc == KC - 1)),
                             reads=[t_wgv[wbi], t_hb], writes=[t_Vp], inc=(kc == KC - 1))
                    eb = ecnt % 2
                    ecnt += 1
                    S.op("act", lambda e, eb=eb, Gb=Gb: e.activation(out=gsb[eb][:, 1:TB + 1], in_=Gb, func=AF.Copy), reads=[t_G[gb]], writes=[t_gsb[eb]])
                    S.op("act", lambda e, eb=eb, Hh=Hh: e.activation(out=gsb[eb][:, 0:1], in_=Hh[:, 0:1], func=AF.Copy), reads=[t_Hh[0]], writes=[t_gsb[eb]])
                    S.op("act", lambda e, eb=eb, Hh=Hh: e.activation(out=gsb[eb][:, TB + 1:TB + 2], in_=Hh[:, 1:2], func=AF.Copy), reads=[t_Hh[0]], writes=[t_gsb[eb]])
                    S.op("dve", lambda e, eb=eb, mc=mc: e.tensor_scalar(out=cac[eb], in0=gsb[eb][:, 0:TB], scalar1=fw[:, 0, mc:mc + 1], scalar2=fw[:, 3, mc:mc + 1],
                                                                        op0=ALU.mult, op1=ALU.add),
                         reads=[t_gsb[eb], t_fw], writes=[t_cac[eb]])
                    for j in (1, 2):
                        S.op("dve", lambda e, eb=eb, mc=mc, j=j: e.scalar_tensor_tensor(out=cac[eb], in0=gsb[eb][:, j:j + TB], scalar=fw[:, j, mc:mc + 1], in1=cac[eb],
                                                                                        op0=ALU.mult, op1=ALU.add),
                             reads=[t_gsb[eb], t_fw, t_cac[eb]], writes=[t_cac[eb]])
                    S.op("act", lambda e, eb=eb: e.activation(out=sg[eb], in_=cac[eb], func=AF.Gelu_apprx_tanh), reads=[t_cac[eb]], writes=[t_sg[eb]])
                    S.op("act", lambda e, eb=eb, Vp=Vp: e.activation(out=vsb[eb], in_=Vp, func=AF.Copy), reads=[t_Vp], writes=[t_vsb[eb]])
                    S.op("dve", lambda e, eb=eb, mc=mc: e.tensor_tensor(out=actT[:, mc, :], in0=vsb[eb], in1=sg[eb], op=ALU.mult),
                         reads=[t_vsb[eb], t_sg[eb]], writes=[t_act[mc]])
                for nb in range(4):
                    for k3 in range(3):
                        wdi = dcnt % 2
                        dcnt += 1
                        S.dma(wd[wdi], wdn_bf[l, nb][:, k3 * 16:(k3 + 1) * 16, :], writes=[t_wd[wdi]], reads=[t_wc[("wdn", l)]])
                        for i in range(4):
                            for kk in range(16):
                                kc = k3 * 16 + kk
                                S.op("pe", lambda e, i=i, kc=kc, kk=kk, wdi=wdi: e.matmul(bank(i), lhsT=actT[:, kc, i * 128:(i + 1) * 128], rhs=wd[wdi][:, kk, :],
                                                                                          start=(kc == 0), stop=(kc == 47)),
                                     reads=[t_act[kc], t_wd[wdi]], writes=[t_acc[i]], inc=(kk == 15))
                    for i in range(4):
                        tq = (nb * 4 + i) % 2
                        S.op("dve", lambda e, i=i, nb=nb, tq=tq: e.tensor_tensor(out=tmp[tq], in0=bank(i), in1=g2[:, nb * 512:(nb + 1) * 512], op=ALU.mult),
                             reads=[t_acc[i], t_g2], writes=[t_tmp[tq]])
                        S.op("pool", lambda e, i=i, nb=nb, tq=tq: e.tensor_tensor(out=xr_[i][:, nb * 512:(nb + 1) * 512], in0=xr_[i][:, nb * 512:(nb + 1) * 512],
                                                                                   in1=tmp[tq], op=ALU.add),
                             reads=[t_tmp[tq], t_xr[i]], writes=[t_xr[i]])
                for i in range(4):
                    r0 = tok0 + i * 128
                    if last:
                        k = n.cnt % 4
                        n.cnt += 1
                        rstd_of(n, xr_[i], t_xr[i], k)
                        S.op("dve", lambda e, i=i, k=k: e.scalar_tensor_tensor(out=xr_[i], in0=xr_[i], scalar=n.rs[k], in1=fn, op0=ALU.mult, op1=ALU.mult),
                             reads=[t_xr[i], n.t_rs[k], t_fn], writes=[t_xr[i]])
                        S.dma(y_out[r0:r0 + 128, :], xr_[i], reads=[t_xr[i]])
                    else:
                        S.dma(xls[r0:r0 + 128, :], xr_[i], reads=[t_xr[i]])
            if os.environ.get("KDEBUG"):
                print("arena words used", A.off, "of", AW)
            S.barrier()
            S.release()

        import os
        stop = os.environ.get("KSTOP", "")
        prologue()
        for l in range(NL):
            if stop == "pro":
                break
            xsrc = x_in if l == 0 else xls
            phase_a(l, xsrc)
            if stop == "a":
                break
            phase_attn(l)
            if stop == "attn1":
                break
            for hd in range(8 if stop not in ("attn1", "rnn1") else 1):
                phase_rnn(l, hd)
            if stop in ("attn1", "rnn1", "b"):
                break
            phase_ca(l, xsrc)
            if stop == "ca":
                break
            phase_cb(l, l == NL - 1)
        S.barrier(final=True)
        S.emit()
    return nc


_CACHE = {}


def _consts():
    return {"ident": np.eye(128, dtype=np.float32).astype(ml_dtypes.bfloat16), "onehot": make_onehot()}


def make_in_map(x_seq, c_seq, weights):
    m = dict(weights)
    m["x"] = np.ascontiguousarray(x_seq, dtype=np.float32)
    m["c"] = np.ascontiguousarray(c_seq, dtype=np.float32).reshape(1, D)
    m.update(_consts())
    return m


def kernel(x_prompt, x_sample, c_prompt, c_sample, w_ada, b_ada, norm1, w_in, w_out, rel_bias,
           rnn_conv_w, rnn_conv_b, w_rg, b_rg, w_ig, b_ig, lam, norm2, w_up, ffn_conv_w, ffn_conv_b,
           w_down, final_norm):
    T = x_prompt.shape[1]
    weights = {"w_ada": w_ada, "b_ada": b_ada, "norm1": norm1, "w_in": w_in, "w_out": w_out, "rel_bias": rel_bias,
               "rnn_conv_w": rnn_conv_w, "rnn_conv_b": rnn_conv_b, "w_rg": w_rg, "b_rg": b_rg, "w_ig": w_ig,
               "b_ig": b_ig, "lam": lam, "norm2": norm2, "w_up": w_up, "ffn_conv_w": ffn_conv_w,
               "ffn_conv_b": ffn_conv_b, "w_down": w_down, "final_norm": np.asarray(final_norm).reshape(1, D)}
    weights = {k: np.ascontiguousarray(np.asarray(v), dtype=np.float32) for k, v in weights.items()}
    xs = [np.asarray(x_prompt)[0]] + [np.asarray(x_sample)[i] for i in range(4)]
    cs = [np.asarray(c_prompt)[0]] + [np.asarray(c_sample)[i] for i in range(4)]
    seq_of_core = [0, 1, 2, 3, 4, 0, 1, 2]
    if "nc" not in _CACHE:
        _CACHE["nc"] = build(T=T, NL=2)
    nc = _CACHE["nc"]
    in_maps = [make_in_map(xs[s], cs[s], weights) for s in seq_of_core]
    res = run_bass_kernel_spmd(nc, in_maps, core_ids=list(range(8)))
    outs = [np.asarray(res.results[i]["y"], dtype=np.float32) for i in range(5)]
    y_prompt = outs[0][None]
    y_sample = np.stack(outs[1:5], axis=0)
    return (y_prompt, y_sample)
```

```python
import math
import os
from contextlib import ExitStack
import numpy as np
import ml_dtypes
import concourse.bass as bass
import concourse.mybir as mybir
from concourse.bass_utils import run_bass_kernel_spmd

F32 = mybir.dt.float32
BF16 = mybir.dt.bfloat16
ALU = mybir.AluOpType
AF = mybir.ActivationFunctionType

D = 2048
KC = 16
NH = 8
DFF = 6144
INW = 5120
EPS = 1e-6
NEG = -30000.0
PATTERNS = ((128, 1), (512, 4), (2048, 16))
GELU_K = 1.5957691216057308


class T_:
    __slots__ = ("w", "r", "dsem", "name", "persist")

    def __init__(self, name="", persist=False):
        self.w = None
        self.r = {}
        self.dsem = None
        self.name = name
        self.persist = persist


class DSem:
    __slots__ = ("h", "count")

    def __init__(self, h):
        self.h = h
        self.count = 0


class Sched:
    ENG = ("pe", "act", "dve", "pool", "sp")

    def __init__(self, nc, stack):
        self.nc = nc
        self.stack = stack
        self.ops = {e: [] for e in self.ENG}
        self.sem = {}
        self.cnt = {e: 0 for e in self.ENG}
        self.known = {e: {} for e in self.ENG}
        for e in self.ENG:
            self.sem[e] = stack.enter_context(nc.semaphore("S_" + e))
        self.nsem = 0
        self.dma_tiles = []
        self.free_ds = []

    def _dsem(self, t):
        if t.dsem is None:
            if self.free_ds:
                t.dsem = self.free_ds.pop()
            else:
                t.dsem = DSem(self.stack.enter_context(self.nc.semaphore("DS%d" % self.nsem)))
                self.nsem += 1
            self.dma_tiles.append(t)
        return t.dsem

    def release(self):
        keep = []
        for t in self.dma_tiles:
            if t.persist:
                keep.append(t)
            else:
                self.free_ds.append(t.dsem)
                t.dsem = None
        self.dma_tiles = keep

    def _wait(self, eng, ev):
        if ev is None:
            return
        sem, val, src = ev
        if src == eng and eng == "pe":
            return
        k = self.known[eng]
        if k.get(id(sem), 0) >= val:
            return
        k[id(sem)] = val
        self.ops[eng].append(lambda e, sem=sem, val=val: e.wait_ge(sem, val))

    def _deps(self, eng, reads, writes):
        for t in reads:
            self._wait(eng, t.w)
        for t in writes:
            self._wait(eng, t.w)
            for ev in t.r.values():
                self._wait(eng, ev)

    def op(self, eng, fn, reads=(), writes=(), inc=True):
        self._deps(eng, reads, writes)
        sem = self.sem[eng]
        ev = (sem, self.cnt[eng] + 1, eng)
        if inc:
            self.cnt[eng] += 1
            self.ops[eng].append(lambda e, fn=fn, sem=sem: fn(e).then_inc(sem, 1))
        else:
            self.ops[eng].append(lambda e, fn=fn: fn(e))
        for t in writes:
            t.w = ev
            t.r = {}
        for t in reads:
            t.r[eng] = ev
        return ev

    def dma(self, out, in_, writes=(), reads=(), q="sp", **kw):
        self._deps(q, reads, writes)
        t = writes[0] if writes else reads[0]
        ds = self._dsem(t)
        ds.count += 16
        sem = ds.h
        ev = (sem, ds.count, "dma")
        self.ops[q].append(
            lambda e, out=out, in_=in_, sem=sem, kw=kw: e.dma_start(out=out, in_=in_, **kw).then_inc(sem, 16))
        for w in writes:
            w.w = ev
            w.r = {}
        for r in reads:
            r.r[("d", id(sem))] = ev
        return ev

    def barrier(self, final=False):
        for e in self.ENG:
            for s in self.ENG:
                if s != e and self.cnt[s] > 0:
                    self._wait(e, (self.sem[s], self.cnt[s], s))
            for t in self.dma_tiles:
                if t.persist and not final:
                    continue
                if t.dsem.count > 0:
                    self._wait(e, (t.dsem.h, t.dsem.count, "dma"))

    def emit(self):
        with self.nc.Block() as block:
            @block.sync
            def _(e):
                for f in self.ops["sp"]:
                    f(e)

            @block.tensor
            def _(e):
                for f in self.ops["pe"]:
                    f(e)

            @block.scalar
            def _(e):
                for f in self.ops["act"]:
                    f(e)

            @block.vector
            def _(e):
                for f in self.ops["dve"]:
                    f(e)

            @block.gpsimd
            def _(e):
                for f in self.ops["pool"]:
                    f(e)


class Arena:
    def __init__(self, ap, words):
        self.a = ap
        self.words = words
        self.off = 0

    def reset(self, to=0):
        self.off = to

    def alloc(self, shape, dt=F32, parts=128):
        n = int(np.prod(shape))
        words = n if dt == F32 else (n + 1) // 2
        words = (words + 7) // 8 * 8
        assert self.off + words <= self.words, ("arena overflow", self.off, words, self.words)
        v = self.a[0:parts, self.off:self.off + words]
        self.off += words
        if dt == BF16:
            v = v.bitcast(BF16)
        v = v[:, 0:n]
        if len(shape) == 2:
            v = v.rearrange("p (a b) -> p a b", a=shape[0])
        elif len(shape) == 3:
            v = v.rearrange("p (a b c) -> p a b c", a=shape[0], b=shape[1])
        elif len(shape) == 4:
            v = v.rearrange("p (a b c d) -> p a b c d", a=shape[0], b=shape[1], c=shape[2])
        return v


def t5_bucket_np(rel):
    nb = 16
    max_exact = 8
    rel = np.asarray(rel, np.int64)
    ret = (rel > 0).astype(np.int64) * nb
    n = np.abs(rel)
    nf = np.maximum(n, 1).astype(np.float32)
    large = max_exact + (np.log(nf / np.float32(max_exact)) / np.float32(math.log(1024 / max_exact))
                         * np.float32(nb - max_exact)).astype(np.int32)
    large = np.minimum(large, nb - 1)
    return ret + np.where(n < max_exact, n, large)


def make_onehot():
    oh = np.zeros((64, 3 * 384), np.float32)
    for p, (win, dil) in enumerate(PATTERNS):
        for e in range(384):
            r = e - 191
            if abs(r) <= 64:
                oh[int(t5_bucket_np(r * dil)), p * 384 + e] = 1.0
            else:
                oh[32, p * 384 + e] = 1.0
    return oh


def build(T=8192, NL=2, dbg=False):
    nc = bass.Bass("TRN2", target_bir_lowering=False)
    NT = T // 128

    def din(name, shape, dt=F32):
        return nc.dram_tensor(name, list(shape), dt, kind="ExternalInput").ap()

    def dscr(name, shape, dt=F32):
        return nc.dram_tensor(name, list(shape), dt, kind=("ExternalOutput" if dbg else "Internal")).ap()

    x_in = din("x", [T, D])
    c_in = din("c", [1, D])
    w_ada = din("w_ada", [2, D, 6 * D])
    b_ada = din("b_ada", [2, 6 * D])
    norm1 = din("norm1", [2, D])
    w_in = din("w_in", [2, D, INW])
    w_out = din("w_out", [2, D, D])
    rel_bias = din("rel_bias", [32, 8])
    rnn_conv_w = din("rnn_conv_w", [2, 4, 1024])
    rnn_conv_b = din("rnn_conv_b", [2, 1024])
    w_rg = din("w_rg", [2, 2, 8, 128, 128])
    b_rg = din("b_rg", [2, 2, 1024])
    w_ig = din("w_ig", [2, 2, 8, 128, 128])
    b_ig = din("b_ig", [2, 2, 1024])
    lam = din("lam", [2, 2, 1024])
    norm2 = din("norm2", [2, D])
    w_up = din("w_up", [2, D, 2 * DFF])
    ffn_conv_w = din("ffn_conv_w", [2, 3, DFF])
    ffn_conv_b = din("ffn_conv_b", [2, DFF])
    w_down = din("w_down", [2, DFF, D])
    final_norm = din("final_norm", [1, D])
    ident_in = din("ident", [128, 128], BF16)
    onehot_in = din("onehot", [64, 3 * 384])
    y_out = nc.dram_tensor("y", [T, D], F32, kind="ExternalOutput").ap()

    win_bf = dscr("win_bf", [NL, 10, 128, KC, 512], BF16)
    wout_bf = dscr("wout_bf", [NL, 128, KC, D], BF16)
    wup_bf = dscr("wup_bf", [NL, 48, 2, 128, KC, 128], BF16)
    wdn_bf = dscr("wdn_bf", [NL, 4, 128, 48, 512], BF16)
    wrg_bf = dscr("wrg_bf", [NL, 2, 8, 128, 128], BF16)
    wig_bf = dscr("wig_bf", [NL, 2, 8, 128, 128], BF16)
    modscr = dscr("modscr", [NL, 6 * D])
    extscr = dscr("extscr", [8, 3 * 384])
    mbscr = dscr("mbscr", [8, 128, 6 * 128])
    qT = dscr("qT", [8, 128, T], BF16)
    kT = dscr("kT", [8, 128, T], BF16)
    vv = dscr("vv", [T, 1024], BF16)
    xrT = dscr("xrT", [8, 128, T])
    grT = dscr("grT", [8, 128, T])
    mixT = dscr("mixT", [16, 128, T], BF16)
    x1s = dscr("x1s", [T, D])
    h2T = dscr("h2T", [16, 128, T], BF16)
    xls = dscr("xls", [T, D])

    with ExitStack() as st:
        S = Sched(nc, st)
        AW = 49152
        arena_t = st.enter_context(nc.sbuf_tensor("arena", [128, AW], F32))
        ps = st.enter_context(nc.psum_tensor("ps", [128, 4096], F32))
        A = Arena(arena_t, AW)

        def bank(b):
            return ps[:, b * 512:(b + 1) * 512]

        ident = A.alloc([128], BF16)
        ones = A.alloc([128], BF16)
        t_ident = T_("ident", persist=True)
        t_ones = T_("ones")
        S.dma(ident, ident_in[:, :], writes=[t_ident])
        S.op("dve", lambda e: e.memset(ones, 1.0), writes=[t_ones])
        A_base = A.off

        t_wc = {}

        def cast_weights(l):
            for nm in ("win", "wrg", "wout", "wup", "wdn"):
                t_wc[(nm, l)] = T_(nm + str(l), persist=True)
            for cb in range(10):
                S.dma(win_bf[l, cb], w_in[l][:, cb * 512:(cb + 1) * 512].rearrange("(kc p) n -> p kc n", p=128),
                      writes=[t_wc[("win", l)]], q="pool")
            S.dma(wrg_bf[l], w_rg[l], writes=[t_wc[("wrg", l)]], q="pool")
            S.dma(wig_bf[l], w_ig[l], writes=[t_wc[("wrg", l)]], q="pool")
            for n4 in range(4):
                S.dma(wout_bf[l][:, :, n4 * 512:(n4 + 1) * 512],
                      w_out[l][:, n4 * 512:(n4 + 1) * 512].rearrange("(kc p) n -> p kc n", p=128),
                      writes=[t_wc[("wout", l)]], q="pool")
            wu = w_up[l].rearrange("(kc p) (gv mc n) -> mc gv p kc n", p=128, gv=2, n=128)
            for mc in range(48):
                for gv in range(2):
                    S.dma(wup_bf[l, mc, gv], wu[mc, gv], writes=[t_wc[("wup", l)]], q="pool")
            for nb in range(4):
                for k3 in range(3):
                    S.dma(wdn_bf[l, nb][:, k3 * 16:(k3 + 1) * 16, :],
                          w_down[l][k3 * 2048:(k3 + 1) * 2048, nb * 512:(nb + 1) * 512].rearrange("(kc p) n -> p kc n", p=128),
                          writes=[t_wc[("wdn", l)]], q="pool")

        import os
        SKIP = os.environ.get("KSKIP", "")
        for l in range(NL):
            if "c" not in SKIP:
                cast_weights(l)

        def prologue():
            A.reset(A_base)
            cT = A.alloc([16])
            scT = A.alloc([16])
            t_c = T_()
            t_sc = T_()
            S.dma(cT, c_in[0, :].rearrange("(kc p) -> p kc", p=128), writes=[t_c], allow_slow_non_contiguous=True)
            S.op("act", lambda e: e.activation(out=scT, in_=cT, func=AF.Silu), reads=[t_c], writes=[t_sc])
            brow = A.alloc([6 * D], parts=1)
            nrow = A.alloc([2 * D], parts=1)
            wt = [A.alloc([KC, 512]) for _ in range(2)]
            t_wt = [T_(), T_()]
            t_brow = T_()
            t_nrow = T_()
            t_pb = [T_(), T_()]
            for l in range(NL if "m" not in SKIP else 0):
                S.dma(brow, b_ada[l:l + 1, :], writes=[t_brow])
                S.dma(nrow[:, 0:D], norm1[l:l + 1, :], writes=[t_nrow])
                S.dma(nrow[:, D:2 * D], norm2[l:l + 1, :], writes=[t_nrow])
                for n in range(24):
                    b = n % 2
                    S.dma(wt[b], w_ada[l][:, n * 512:(n + 1) * 512].rearrange("(kc p) n -> p kc n", p=128), writes=[t_wt[b]])
                    for kc in range(KC):
                        S.op("pe", lambda e, b=b, kc=kc: e.matmul(bank(b)[0:1, :], lhsT=scT[:, kc:kc + 1], rhs=wt[b][:, kc, :],
                                                                   start=(kc == 0), stop=(kc == KC - 1)),
                             reads=[t_sc, t_wt[b]], writes=[t_pb[b]], inc=(kc == KC - 1))
                    S.op("dve", lambda e, b=b, n=n: e.tensor_tensor(out=brow[:, n * 512:(n + 1) * 512], in0=bank(b)[0:1, :],
                                                                     in1=brow[:, n * 512:(n + 1) * 512], op=ALU.add),
                         reads=[t_pb[b]], writes=[t_brow])
                S.op("dve", lambda e: e.scalar_tensor_tensor(out=brow[:, D:2 * D], in0=brow[:, D:2 * D], scalar=1.0,
                                                             in1=nrow[:, 0:D], op0=ALU.add, op1=ALU.mult),
                     reads=[t_nrow], writes=[t_brow])
                S.op("dve", lambda e: e.scalar_tensor_tensor(out=brow[:, 4 * D:5 * D], in0=brow[:, 4 * D:5 * D], scalar=1.0,
                                                             in1=nrow[:, D:2 * D], op0=ALU.add, op1=ALU.mult),
                     reads=[t_nrow], writes=[t_brow])
                S.dma(modscr[l:l + 1, :], brow, reads=[t_brow])
            tab = A.alloc([8], parts=64)
            oh = A.alloc([3 * 384], parts=64)
            ext = A.alloc([3 * 384], parts=8)
            t_tab = T_()
            t_oh = T_()
            t_ext = T_()
            S.op("dve", lambda e: e.memset(tab[32:64, :], NEG), writes=[t_tab])
            S.dma(tab[0:32, :], rel_bias[:, :], writes=[t_tab])
            S.dma(oh, onehot_in[:, :], writes=[t_oh])
            for p in range(3 if "o" not in SKIP else 0):
                S.op("pe", lambda e, p=p: e.matmul(bank(2 + p)[0:8, 0:384], lhsT=tab, rhs=oh[:, p * 384:(p + 1) * 384],
                                                   start=True, stop=True),
                     reads=[t_tab, t_oh], writes=[t_pb[0]])
                S.op("dve", lambda e, p=p: e.tensor_copy(out=ext[:, p * 384:(p + 1) * 384], in_=bank(2 + p)[0:8, 0:384]),
                     reads=[t_pb[0]], writes=[t_ext])
            S.dma(extscr[:, :], ext, reads=[t_ext])
            if os.environ.get("KDEBUG"):
                print("arena words used", A.off, "of", AW)
            S.barrier()
            S.release()
            Hk = [A.alloc([6, 128]) for _ in range(2)]
            Tm = [A.alloc([6, 128]) for _ in range(2)]
            MBt = [A.alloc([6, 128]) for _ in range(2)]
            t_H = [T_(), T_()]
            t_Tm = [T_(), T_()]
            t_MB = [T_(), T_()]
            for hd in range(8 if "h" not in SKIP else 0):
                b = hd % 2
                src = bass.AP(extscr.tensor, hd * 3 * 384, [[1, 128], [128, 6], [1, 128]])
                for p in range(3):
                    srcp = bass.AP(extscr.tensor, hd * 3 * 384 + p * 384, [[1, 128], [128, 2], [1, 128]])
                    S.dma(Hk[b][:, 2 * p:2 * p + 2, :], srcp, writes=[t_H[b]])
                S.op("dve", lambda e, b=b: e.tensor_copy(out=Tm[b], in_=Hk[b][:, :, ::-1]), reads=[t_H[b]], writes=[t_Tm[b]])
                S.op("act", lambda e, b=b: e.activation(out=MBt[b], in_=Tm[b], func=AF.Exp), reads=[t_Tm[b]], writes=[t_MB[b]])
                S.dma(mbscr[hd], MBt[b].rearrange("p a b -> p (a b)"), reads=[t_MB[b]])
            if os.environ.get("KDEBUG"):
                print("arena words used", A.off, "of", AW)
            S.barrier()
            S.release()

        def load_mod_cols(l, idx_gm, idx_sh):
            gmT = A.alloc([16])
            shT = A.alloc([16])
            t_g = T_()
            S.dma(gmT, modscr[l, idx_gm * D:(idx_gm + 1) * D].rearrange("(kc p) -> p kc", p=128), writes=[t_g],
                  allow_slow_non_contiguous=True)
            S.dma(shT, modscr[l, idx_sh * D:(idx_sh + 1) * D].rearrange("(kc p) -> p kc", p=128), writes=[t_g],
                  allow_slow_non_contiguous=True)
            return gmT, shT, t_g

        def bcast_row(dst, src_tensor_ap, offset, t_dst):
            S.dma(dst, bass.AP(src_tensor_ap.tensor, offset, [[0, 128], [1, D]]), writes=[t_dst])

        class NormCtx:
            pass

        def make_norm_ctx():
            n = NormCtx()
            n.sq = A.alloc([D], BF16)
            n.t_sq = T_()
            n.ss = [A.alloc([1]) for _ in range(4)]
            n.rs = [A.alloc([1]) for _ in range(4)]
            n.t_ss = [T_() for _ in range(4)]
            n.t_rs = [T_() for _ in range(4)]
            n.xn = [A.alloc([D], BF16) for _ in range(2)]
            n.t_xn = [T_(), T_()]
            n.t_tp = [T_() for _ in range(4)]
            n.cnt = 0
            return n

        def tp_region(r):
            base = r * 512
            return ps[:, base:base + 256].bitcast(BF16).rearrange("p (a b) -> p a b", a=4)

        def rstd_of(n, src, t_src, k):
            S.op("act", lambda e: e.activation(out=n.sq, in_=src, func=AF.Square, scale=1.0 / math.sqrt(D), accum_out=n.ss[k]),
                 reads=[t_src], writes=[n.t_sq, n.t_ss[k]])
            S.op("act", lambda e: e.activation(out=n.rs[k], in_=n.ss[k], func=AF.Sqrt, bias=EPS, scale=1.0),
                 reads=[n.t_ss[k]], writes=[n.t_rs[k]])
            S.op("dve", lambda e: e.reciprocal(out=n.rs[k], in_=n.rs[k]), reads=[n.t_rs[k]], writes=[n.t_rs[k]])

        def norm_mod_transpose(n, src, t_src, gmT, shT, t_g, dst_fn, t_dst_fn, defer=False):
            k = n.cnt % 4
            b = n.cnt % 2
            n.cnt += 1
            rstd_of(n, src, t_src, k)
            S.op("dve", lambda e: e.tensor_scalar(out=n.xn[b], in0=src, scalar1=n.rs[k], scalar2=None, op0=ALU.mult),
                 reads=[t_src, n.t_rs[k]], writes=[n.t_xn[b]])

            def part2():
                _nmt_part2(n, b, gmT, shT, t_g, dst_fn, t_dst_fn)
            if defer:
                return part2
            part2()
            return None

        def _nmt_part2(n, b, gmT, shT, t_g, dst_fn, t_dst_fn):
            for g4 in range(4):
                r = g4
                reg = tp_region(r)
                for j in range(4):
                    kc = g4 * 4 + j
                    S.op("pe", lambda e, reg=reg, j=j, kc=kc, b=b: e.transpose(reg[:, j, :], n.xn[b][:, kc * 128:(kc + 1) * 128], ident),
                         reads=[n.t_xn[b], t_ident], writes=[n.t_tp[r]], inc=(j == 3))
                for j in range(4):
                    kc = g4 * 4 + j
                    if g4 % 2 == 0:
                        S.op("act", lambda e, reg=reg, j=j, kc=kc: e.activation(out=dst_fn(kc), in_=reg[:, j, :], func=AF.Identity,
                                                                                 scale=gmT[:, kc:kc + 1], bias=shT[:, kc:kc + 1]),
                             reads=[n.t_tp[r], t_g], writes=[t_dst_fn(kc)])
                    else:
                        S.op("dve", lambda e, reg=reg, j=j, kc=kc: e.tensor_scalar(out=dst_fn(kc), in0=reg[:, j, :], scalar1=gmT[:, kc:kc + 1],
                                                                                   scalar2=shT[:, kc:kc + 1], op0=ALU.mult, op1=ALU.add),
                             reads=[n.t_tp[r], t_g], writes=[t_dst_fn(kc)])

        def phase_a(l, xsrc):
            A.reset(A_base)
            TB = 1024 if T >= 1024 else T
            NTI = TB // 128
            NJ = TB // 512
            gmT, shT, t_g = load_mod_cols(l, 1, 0)
            n = make_norm_ctx()
            xt = [A.alloc([D]) for _ in range(2)]
            t_xt = [T_(), T_()]
            hT2 = [A.alloc([KC, TB], BF16) for _ in range(2)]
            t_h2b = [[[T_() for _ in range(NTI)] for _ in range(KC)] for _ in range(2)]
            wb = [A.alloc([KC, 512], BF16) for _ in range(2)]
            t_wb = [T_(), T_()]
            ob = [A.alloc([512]) for _ in range(4)]
            t_ob = [T_() for _ in range(4)]
            t_bk = [T_() for _ in range(4)]
            gcnt = [0]
            scale_q = 1.0 / math.sqrt(128.0)
            wcnt = [0]

            def load_w(cb):
                b = wcnt[0] % 2
                wcnt[0] += 1
                S.dma(wb[b], win_bf[l, cb], writes=[t_wb[b]], reads=[t_wc[("win", l)]])
                return b

            xcnt = [0]

            def norm_tile(tbn, i, defer):
                tok0 = tbn * TB + i * 128
                b = xcnt[0] % 2
                xcnt[0] += 1
                hbuf = hT2[tbn % 2]
                thb = t_h2b[tbn % 2]
                S.dma(xt[b], xsrc[tok0:tok0 + 128, :], writes=[t_xt[b]])
                return norm_mod_transpose(n, xt[b], t_xt[b], gmT, shT, t_g,
                                          lambda kc, i=i, hbuf=hbuf: hbuf[:, kc, i * 128:(i + 1) * 128], lambda kc, i=i, thb=thb: thb[kc][i],
                                          defer=defer)

            NBLK = T // TB
            for i in range(NTI):
                norm_tile(0, i, False)
            for tb in range(NBLK):
                nxt_w = load_w(0)
                hT = hT2[tb % 2]
                t_h = t_h2b[tb % 2]
                pend = None
                cbs = list(range(10))
                for cb in cbs:
                    if pend is not None:
                        pend()
                        pend = None
                    if tb + 1 < NBLK and cb < NTI:
                        pend = norm_tile(tb + 1, cb, True)
                    wbi = nxt_w
                    if cb + 1 < 10:
                        nxt_w = load_w(cb + 1)
                    W = wb[wbi]
                    if cb in (4, 5):
                        for i in range(NTI):
                            tok0 = tb * TB + i * 128
                            g = gcnt[0]
                            gcnt[0] += 1
                            bk = 4 + g % 4
                            for kc in range(KC):
                                S.op("pe", lambda e, bk=bk, kc=kc, i=i, W=W, hT=hT: e.matmul(bank(bk), lhsT=hT[:, kc, i * 128:(i + 1) * 128], rhs=W[:, kc, :],
                                                                                       start=(kc == 0), stop=(kc == KC - 1)),
                                     reads=[t_h[kc][i], t_wb[wbi]], writes=[t_bk[bk - 4]], inc=(kc == KC - 1))
                            o = g % 4
                            obv = ob[o].bitcast(BF16)[:, 0:512]
                            eng = "act" if bk % 2 == 0 else "dve"
                            if eng == "act":
                                S.op("act", lambda e, bk=bk, obv=obv: e.activation(out=obv, in_=bank(bk), func=AF.Copy),
                                     reads=[t_bk[bk - 4]], writes=[t_ob[o]])
                            else:
                                S.op("dve", lambda e, bk=bk, obv=obv: e.tensor_copy(out=obv, in_=bank(bk)),
                                     reads=[t_bk[bk - 4]], writes=[t_ob[o]])
                            S.dma(vv[tok0:tok0 + 128, (cb - 4) * 512:(cb - 3) * 512], obv, reads=[t_ob[o]])
                    else:
                        for m in range(4):
                            f = cb * 512 + m * 128
                            kind = f // 1024
                            hd = (f % 1024) // 128
                            for j in range(NJ):
                                g = gcnt[0]
                                gcnt[0] += 1
                                bk = 4 + g % 4
                                for kc in range(KC):
                                    S.op("pe", lambda e, bk=bk, kc=kc, m=m, j=j, W=W, hT=hT: e.matmul(bank(bk), lhsT=W[:, kc, m * 128:(m + 1) * 128],
                                                                                                rhs=hT[:, kc, j * 512:(j + 1) * 512],
                                                                                                start=(kc == 0), stop=(kc == KC - 1)),
                                         reads=[t_h[kc][j * 4 + q4] for q4 in range(4)] + [t_wb[wbi]], writes=[t_bk[bk - 4]], inc=(kc == KC - 1))
                                o = g % 4
                                tk0 = tb * TB + j * 512
                                if kind <= 1:
                                    obv = ob[o].bitcast(BF16)[:, 0:512]
                                    dst = (qT if kind == 0 else kT)[hd][:, tk0:tk0 + 512]
                                    sc = scale_q if kind == 0 else 1.0
                                else:
                                    obv = ob[o]
                                    dst = (xrT if kind == 3 else grT)[hd][:, tk0:tk0 + 512]
                                    sc = 1.0
                                if bk % 2 == 0:
                                    S.op("act", lambda e, bk=bk, obv=obv, sc=sc: e.activation(out=obv, in_=bank(bk), func=AF.Copy, scale=sc),
                                         reads=[t_bk[bk - 4]], writes=[t_ob[o]])
                                else:
                                    S.op("dve", lambda e, bk=bk, obv=obv, sc=sc: e.tensor_scalar(out=obv, in0=bank(bk), scalar1=sc, scalar2=None, op0=ALU.mult),
                                         reads=[t_bk[bk - 4]], writes=[t_ob[o]])
                                S.dma(dst, obv, reads=[t_ob[o]])
            if os.environ.get("KDEBUG"):
                print("arena words used", A.off, "of", AW)
            S.barrier()
            S.release()

        def phase_attn(l):
            A.reset(A_base)
            NCH = T // 128
            qs2 = [A.alloc([T], BF16) for _ in range(2)]
            ks2 = [A.alloc([T], BF16) for _ in range(2)]
            t_q2 = [T_(), T_()]
            t_k2 = [T_(), T_()]
            Eh2 = [A.alloc([6, 128]) for _ in range(2)]
            t_E2 = [T_(), T_()]
            Vb = [A.alloc([NCH, 128], BF16) for _ in range(2)]
            t_V = [T_(), T_()]
            num = A.alloc([T])
            den = A.alloc([T])
            t_num = T_()
            t_den = T_()
            ao = A.alloc([T], BF16)
            t_ao = T_()
            PT = [A.alloc([2, 128], BF16) for _ in range(4)]
            t_PT = [T_() for _ in range(4)]
            XS = [A.alloc([2, 128]) for _ in range(4)]
            t_XS = [T_() for _ in range(4)]
            t_S = [T_() for _ in range(4)]
            t_O = [T_(), T_()]
            t_Dn = [T_(), T_()]

            def load_head(hd):
                hb = hd % 2
                S.dma(qs2[hb], qT[hd], writes=[t_q2[hb]])
                S.dma(ks2[hb], kT[hd], writes=[t_k2[hb]])
                S.dma(Eh2[hb].rearrange("p a b -> p (a b)"), mbscr[hd], writes=[t_E2[hb]])

            vlist = [(hd, p) for hd in range(8) for p in range(3)]

            def load_v(vi):
                hd, p = vlist[vi]
                dil = PATTERNS[p][1]
                nchr = (T // dil) // 128
                vsrc = bass.AP(vv.tensor, hd * 128, [[dil * 1024, 128], [1024, dil], [128 * dil * 1024, nchr], [1, 128]])
                S.dma(Vb[vi % 2].rearrange("p (r c) d -> p r c d", r=dil), vsrc, writes=[t_V[vi % 2]])

            def Sreg(r):
                base = (0, 1, 6, 7)[r] * 512
                return ps[:, base:base + 256].rearrange("p (a b) -> p a b", a=2)

            tcnt = [0]
            gcnt = [0]

            def sl(j0, n, r, dil):
                s0 = r + dil * j0
                return slice(s0, s0 + dil * (n - 1) + 1, dil)

            def do_tile(hb, p, r, dil, vb, jq0, nq, chunks, mb0, mbcol0, ocol, gb):
                qs, ks, Eh = qs2[hb], ks2[hb], Eh2[hb]
                t_q, t_k, t_E = t_q2[hb], t_k2[hb], t_E2[hb]
                L = T // dil
                nchr = L // 128
                ti = tcnt[0] % 4
                tcnt[0] += 1
                Sr = Sreg(ti)
                qsl = sl(jq0, nq, r, dil)
                ncu = len(chunks)
                for ci, kc_ in enumerate(chunks):
                    ksl = sl(kc_ * 128, 128, r, dil)
                    S.op("pe", lambda e, Sr=Sr, ci=ci, ksl=ksl, qsl=qsl, nq=nq, ks=ks, qs=qs: e.matmul(Sr[:, ci, 0:nq], lhsT=ks[:, ksl], rhs=qs[:, qsl], start=True, stop=True),
                         reads=[t_k, t_q], writes=[t_S[ti]], inc=(ci == ncu - 1))
                S.op("act", lambda e, Sr=Sr, ti=ti, ncu=ncu, nq=nq: e.activation(out=XS[ti][:, 0:ncu, 0:nq], in_=Sr[:, 0:ncu, 0:nq], func=AF.Exp),
                     reads=[t_S[ti]], writes=[t_XS[ti]])
                e0 = p * 2 + mb0
                S.op("pool", lambda e, ti=ti, ncu=ncu, nq=nq, e0=e0, Eh=Eh: e.tensor_tensor(out=PT[ti][:, 0:ncu, 0:nq], in0=XS[ti][:, 0:ncu, 0:nq],
                                                                                         in1=Eh[:, e0:e0 + ncu, mbcol0:mbcol0 + nq], op=ALU.mult),
                     reads=[t_XS[ti], t_E], writes=[t_PT[ti]])
                ob_ = bank(2 + gb)
                db_ = bank(4 + gb)

                def partB():
                    for ci, kc_ in enumerate(chunks):
                        vch = r * nchr + kc_
                        S.op("pe", lambda e, ci=ci, vch=vch: e.matmul(ob_[:, ocol:ocol + nq], lhsT=Vb[vb][:, vch, :], rhs=PT[ti][:, ci, 0:nq],
                                                                       start=(ci == 0), stop=(ci == ncu - 1)),
                             reads=[t_V[vb], t_PT[ti]], writes=[t_O[gb]], inc=False)
                    for ci, kc_ in enumerate(chunks):
                        S.op("pe", lambda e, ci=ci: e.matmul(db_[:, ocol:ocol + nq], lhsT=ones, rhs=PT[ti][:, ci, 0:nq],
                                                              start=(ci == 0), stop=(ci == ncu - 1)),
                             reads=[t_ones, t_PT[ti]], writes=[t_Dn[gb]], inc=(ci == ncu - 1))
                fifo.append(("B", partB))
                drain(3)

            fifo = []

            def drain(keep):
                while sum(1 for k_, _ in fifo if k_ == "B") > keep:
                    k_, f_ = fifo.pop(0)
                    f_()
                    while fifo and fifo[0][0] == "F":
                        fifo.pop(0)[1]()

            def flush(gb, ocol0, nq, jq0, r, dil, first):
                fifo.append(("F", lambda: flush_now(gb, ocol0, nq, jq0, r, dil, first)))

            def flush_now(gb, ocol0, nq, jq0, r, dil, first):
                csl = sl(jq0, nq, r, dil)
                ob_ = bank(2 + gb)
                db_ = bank(4 + gb)
                if first:
                    S.op("dve", lambda e: e.tensor_copy(out=num[:, csl], in_=ob_[:, ocol0:ocol0 + nq]), reads=[t_O[gb]], writes=[t_num])
                    S.op("dve", lambda e: e.tensor_copy(out=den[:, csl], in_=db_[:, ocol0:ocol0 + nq]), reads=[t_Dn[gb]], writes=[t_den])
                else:
                    S.op("dve", lambda e: e.tensor_tensor(out=num[:, csl], in0=ob_[:, ocol0:ocol0 + nq], in1=num[:, csl], op=ALU.add),
                         reads=[t_O[gb]], writes=[t_num])
                    S.op("dve", lambda e: e.tensor_tensor(out=den[:, csl], in0=db_[:, ocol0:ocol0 + nq], in1=den[:, csl], op=ALU.add),
                         reads=[t_Dn[gb]], writes=[t_den])

            load_head(0)
            load_v(0)
            nheads = 8 if os.environ.get("KSTOP", "") not in ("attn1", "rnn1") else 1
            for hd in range(nheads):
                hb = hd % 2
                if hd + 1 < nheads:
                    load_head(hd + 1)
                for p, (win, dil) in enumerate(PATTERNS):
                    vi = hd * 3 + p
                    drain(0)
                    while fifo:
                        fifo.pop(0)[1]()
                    if vi + 1 < nheads * 3:
                        load_v(vi + 1)
                    L = T // dil
                    nchr = L // 128
                    vb = vi % 2
                    first = (p == 0)
                    for r in range(dil):
                        c = 0
                        while c < nchr - 1:
                            gsz = min(4, nchr - 1 - c)
                            gb = gcnt[0] % 2
                            gcnt[0] += 1
                            for gi in range(gsz):
                                cc = c + gi
                                do_tile(hb, p, r, dil, vb, 64 + 128 * cc, 128, [cc, cc + 1], 0, 0, gi * 128, gb)
                            flush(gb, 0, gsz * 128, 64 + 128 * c, r, dil, first)
                            c += gsz
                        gb = gcnt[0] % 2
                        gcnt[0] += 1
                        do_tile(hb, p, r, dil, vb, 0, 64, [0], 1, 64, 0, gb)
                        do_tile(hb, p, r, dil, vb, L - 64, 64, [nchr - 1], 0, 0, 64, gb)
                        flush(gb, 0, 64, 0, r, dil, first)
                        flush(gb, 64, 64, L - 64, r, dil, first)
                drain(0)
                while fifo:
                    fifo.pop(0)[1]()
                S.op("dve", lambda e: e.reciprocal(out=den, in_=den), reads=[t_den], writes=[t_den])
                S.op("dve", lambda e: e.tensor_tensor(out=ao, in0=num, in1=den, op=ALU.mult), reads=[t_num, t_den], writes=[t_ao])
                S.dma(mixT[hd], ao, reads=[t_ao])
            if os.environ.get("KDEBUG"):
                print("arena words used", A.off, "of", AW)
            S.barrier()
            S.release()

        def phase_rnn(l):
            A.reset(A_base)
            CH = min(2048, T)
            NCk = T // CH
            nb4 = CH // 512
            u = A.alloc([T])
            ubf = A.alloc([T], BF16)
            yacc = A.alloc([T])
            t_u = [T_() for _ in range(NCk)]
            t_ub = [T_() for _ in range(NCk)]
            t_y = [T_() for _ in range(NCk)]
            xp2 = [A.alloc([CH + 3]) for _ in range(2)]
            t_xp2 = [T_(), T_()]
            rt2 = [A.alloc([CH]) for _ in range(2)]
            st2 = [A.alloc([CH]) for _ in range(2)]
            it2 = [A.alloc([CH]) for _ in range(2)]
            hs2 = [A.alloc([CH]) for _ in range(2)]
            gr2 = [A.alloc([CH]) for _ in range(2)]
            t_rt2, t_st2, t_it2, t_hs2, t_gr2 = ([T_(), T_()] for _ in range(5))
            obf2 = [A.alloc([CH], BF16) for _ in range(2)]
            t_obf2 = [T_(), T_()]
            wg2 = [A.alloc([4, 128], BF16) for _ in range(2)]
            bg2 = [A.alloc([4]) for _ in range(2)]
            cw2 = [A.alloc([5]) for _ in range(2)]
            lm2 = [A.alloc([2]) for _ in range(2)]
            cl2 = [A.alloc([4]) for _ in range(2)]
            t_wg2 = [T_(), T_()]
            t_sm2 = [T_(), T_()]
            t_cl2 = [T_(), T_()]
            carry = A.alloc([1])
            t_carry = T_()
            t_gb = [T_() for _ in range(8)]
            cctr = [0]
            xctr = [0]
            nheads = 8 if os.environ.get("KSTOP", "") not in ("attn1", "rnn1") else 1

            def load_params(hd):
                hb = hd % 2
                wg, bg, cw, lm, cl = wg2[hb], bg2[hb], cw2[hb], lm2[hb], cl2[hb]
                hsl = slice(hd * 128, (hd + 1) * 128)
                for d in range(2):
                    S.dma(wg[:, d, :], wrg_bf[l, d, hd], writes=[t_wg2[hb]], reads=[t_wc[("wrg", l)]])
                    S.dma(wg[:, 2 + d, :], wig_bf[l, d, hd], writes=[t_wg2[hb]], reads=[t_wc[("wrg", l)]])
                    S.dma(bg[:, d:d + 1], b_rg[l, d, hsl].rearrange("(p o) -> p o", o=1), writes=[t_sm2[hb]])
                    S.dma(bg[:, 2 + d:3 + d], b_ig[l, d, hsl].rearrange("(p o) -> p o", o=1), writes=[t_sm2[hb]])
                    S.dma(lm[:, d:d + 1], lam[l, d, hsl].rearrange("(p o) -> p o", o=1), writes=[t_sm2[hb]])
                for j in range(4):
                    S.dma(cw[:, j:j + 1], rnn_conv_w[l, j, hsl].rearrange("(p o) -> p o", o=1), writes=[t_sm2[hb]])
                S.dma(cw[:, 4:5], rnn_conv_b[l, hsl].rearrange("(p o) -> p o", o=1), writes=[t_sm2[hb]])
                S.op("act", lambda e, cl=cl, lm=lm: e.activation(out=cl[:, 0:2], in_=lm, func=AF.Exp, scale=-1.0), reads=[t_sm2[hb]], writes=[t_cl2[hb]])
                S.op("act", lambda e, cl=cl: e.activation(out=cl[:, 0:2], in_=cl[:, 0:2], func=AF.Ln, bias=1.0, scale=1.0), reads=[t_cl2[hb]], writes=[t_cl2[hb]])
                S.op("dve", lambda e, cl=cl: e.tensor_scalar(out=cl[:, 2:4], in0=cl[:, 0:2], scalar1=-16.0, scalar2=None, op0=ALU.mult), reads=[t_cl2[hb]], writes=[t_cl2[hb]])
                S.op("dve", lambda e, cl=cl: e.tensor_scalar(out=cl[:, 0:2], in0=cl[:, 0:2], scalar1=-8.0, scalar2=None, op0=ALU.mult), reads=[t_cl2[hb]], writes=[t_cl2[hb]])

            load_params(0)
            for hd in range(nheads):
                hb = hd % 2
                wg, bg, cw, cl = wg2[hb], bg2[hb], cw2[hb], cl2[hb]
                t_wg, t_sm, t_cl = t_wg2[hb], t_sm2[hb], t_cl2[hb]
                if hd + 1 < nheads:
                    load_params(hd + 1)
                fwd_first = (hd % 2 == 0)
                dirs = (0, 1) if fwd_first else (1, 0)
                conv_order = list(range(NCk)) if fwd_first else list(range(NCk - 1, -1, -1))
                for ck in conv_order:
                    c0 = ck * CH
                    lo = max(c0 - 2, 0)
                    hi = min(c0 + CH + 1, T)
                    xb = xctr[0] % 2
                    xctr[0] += 1
                    xp = xp2[xb]
                    t_xp = t_xp2[xb]
                    if ck == 0:
                        S.op("dve", lambda e, xp=xp: e.memset(xp[:, 0:2], 0.0), writes=[t_xp])
                    if ck == NCk - 1:
                        S.op("dve", lambda e, xp=xp: e.memset(xp[:, CH + 2:CH + 3], 0.0), writes=[t_xp])
                    S.dma(xp[:, lo - (c0 - 2):hi - (c0 - 2)], xrT[hd][:, lo:hi], writes=[t_xp])
                    uc = u[:, c0:c0 + CH]
                    S.op("dve", lambda e, uc=uc, xp=xp, cw=cw: e.tensor_scalar(out=uc, in0=xp[:, 0:CH], scalar1=cw[:, 0:1], scalar2=cw[:, 4:5], op0=ALU.mult, op1=ALU.add),
                         reads=[t_xp, t_sm], writes=[t_u[ck]])
                    for j in range(1, 4):
                        S.op("dve", lambda e, uc=uc, j=j, xp=xp, cw=cw: e.scalar_tensor_tensor(out=uc, in0=xp[:, j:j + CH], scalar=cw[:, j:j + 1], in1=uc, op0=ALU.mult, op1=ALU.add),
                             reads=[t_xp, t_sm, t_u[ck]], writes=[t_u[ck]])
                    S.op("act", lambda e, uc=uc, c0=c0: e.activation(out=ubf[:, c0:c0 + CH], in_=uc, func=AF.Copy), reads=[t_u[ck]], writes=[t_ub[ck]])
                for si, d in enumerate(dirs):
                    order = range(NCk) if d == 0 else range(NCk - 1, -1, -1)
                    for oi, ck in enumerate(order):
                        c0 = ck * CH
                        pb = cctr[0] % 2
                        cctr[0] += 1
                        rt, s_t, it, hs, grt, obf = rt2[pb], st2[pb], it2[pb], hs2[pb], gr2[pb], obf2[pb]
                        t_rt, t_st, t_it, t_hs, t_gr, t_obf = t_rt2[pb], t_st2[pb], t_it2[pb], t_hs2[pb], t_gr2[pb], t_obf2[pb]
                        if si == 1:
                            S.dma(grt, grT[hd][:, c0:c0 + CH], writes=[t_gr])
                        for gi, (wi, dst, t_dst) in enumerate(((d, rt, t_rt), (2 + d, it, t_it))):
                            for q4 in range(nb4):
                                bk = gi * 4 + q4
                                S.op("pe", lambda e, bk=bk, wi=wi, q4=q4, c0=c0, wg=wg: e.matmul(bank(bk), lhsT=wg[:, wi, :], rhs=ubf[:, c0 + q4 * 512:c0 + (q4 + 1) * 512],
                                                                                               start=True, stop=True),
                                     reads=[t_wg, t_ub[ck]], writes=[t_gb[bk]])
                                S.op("act", lambda e, bk=bk, wi=wi, q4=q4, dst=dst, bg=bg: e.activation(out=dst[:, q4 * 512:(q4 + 1) * 512], in_=bank(bk), func=AF.Sigmoid,
                                                                                                     bias=bg[:, wi:wi + 1], scale=1.0),
                                     reads=[t_gb[bk], t_sm], writes=[t_dst])
                        S.op("act", lambda e, rt=rt, s_t=s_t, d=d, cl=cl: e.activation(out=s_t, in_=rt, func=AF.Exp, scale=cl[:, 2 + d:3 + d]), reads=[t_rt, t_cl], writes=[t_st])
                        S.op("act", lambda e, rt=rt, d=d, cl=cl: e.activation(out=rt, in_=rt, func=AF.Exp, scale=cl[:, d:d + 1]), reads=[t_rt, t_cl], writes=[t_rt])
                        S.op("act", lambda e, s_t=s_t: e.activation(out=s_t, in_=s_t, func=AF.Sqrt, bias=1.0, scale=-1.0), reads=[t_st], writes=[t_st])
                        if si == 1:
                            S.op("act", lambda e, grt=grt: e.activation(out=grt, in_=grt, func=AF.Gelu_apprx_tanh), reads=[t_gr], writes=[t_gr])
                        S.op("dve", lambda e, it=it, c0=c0: e.tensor_tensor(out=it, in0=it, in1=u[:, c0:c0 + CH], op=ALU.mult), reads=[t_it, t_u[ck]], writes=[t_it])
                        S.op("dve", lambda e, it=it, s_t=s_t: e.tensor_tensor(out=it, in0=it, in1=s_t, op=ALU.mult), reads=[t_it, t_st], writes=[t_it])
                        init = 0.0 if oi == 0 else carry
                        yc = yacc[:, c0:c0 + CH]
                        dst_s = yc if si == 0 else hs
                        t_dst_s = t_y[ck] if si == 0 else t_hs
                        if d == 0:
                            S.op("dve", lambda e, rt=rt, it=it, dst_s=dst_s, init=init: e.tensor_tensor_scan(out=dst_s, data0=rt, data1=it, initial=init, op0=ALU.mult, op1=ALU.add),
                                 reads=[t_rt, t_it, t_carry], writes=[t_dst_s])
                            S.op("dve", lambda e, dst_s=dst_s: e.tensor_copy(out=carry, in_=dst_s[:, CH - 1:CH]), reads=[t_dst_s], writes=[t_carry])
                        else:
                            S.op("dve", lambda e, rt=rt, it=it, dst_s=dst_s, init=init: e.tensor_tensor_scan(out=dst_s[:, ::-1], data0=rt[:, ::-1], data1=it[:, ::-1], initial=init,
                                                                                                         op0=ALU.mult, op1=ALU.add),
                                 reads=[t_rt, t_it, t_carry], writes=[t_dst_s])
                            S.op("dve", lambda e, dst_s=dst_s: e.tensor_copy(out=carry, in_=dst_s[:, 0:1]), reads=[t_dst_s], writes=[t_carry])
                        if si == 1:
                            S.op("dve", lambda e, yc=yc, hs=hs: e.tensor_tensor(out=hs, in0=yc, in1=hs, op=ALU.add), reads=[t_hs, t_y[ck]], writes=[t_hs])
                            S.op("dve", lambda e, hs=hs, grt=grt, obf=obf: e.tensor_tensor(out=obf, in0=hs, in1=grt, op=ALU.mult), reads=[t_hs, t_gr], writes=[t_obf])
                            S.dma(mixT[8 + hd][:, c0:c0 + CH], obf, reads=[t_obf])
            if os.environ.get("KDEBUG"):
                print("arena words used", A.off, "of", AW)
            S.barrier()
            S.release()

        def phase_ca(l, xsrc):
            A.reset(A_base)
            TB = 512
            gmT, shT, t_g = load_mod_cols(l, 4, 3)
            n = make_norm_ctx()
            wo = A.alloc([KC, D], BF16)
            t_wo = T_()
            for n4 in range(4):
                S.dma(wo[:, :, n4 * 512:(n4 + 1) * 512], wout_bf[l][:, :, n4 * 512:(n4 + 1) * 512], writes=[t_wo], reads=[t_wc[("wout", l)]])
            g1 = A.alloc([D])
            t_g1 = T_()
            bcast_row(g1, modscr, l * 6 * D + 2 * D, t_g1)
            mx = [A.alloc([KC, TB], BF16) for _ in range(2)]
            t_mx = [T_(), T_()]
            xt = [A.alloc([D]) for _ in range(2)]
            t_xt = [T_(), T_()]
            x1 = [A.alloc([D]) for _ in range(2)]
            t_x1 = [T_(), T_()]
            h2 = [A.alloc([KC, TB], BF16) for _ in range(2)]
            t_h2 = [[[T_() for _ in range(4)] for _ in range(KC)] for _ in range(2)]
            t_bk = [T_() for _ in range(4)]
            NB = T // TB
            S.dma(mx[0], mixT[:, :, 0:TB].rearrange("k p t -> p k t"), writes=[t_mx[0]])
            cnt = 0
            pend = [None]
            pstore = [None]
            for tb in range(NB):
                mb = tb % 2
                if tb + 1 < NB:
                    S.dma(mx[1 - mb], mixT[:, :, (tb + 1) * TB:(tb + 2) * TB].rearrange("k p t -> p k t"), writes=[t_mx[1 - mb]])
                for i in range(4):
                    tok0 = tb * TB + i * 128
                    b = cnt % 2
                    cnt += 1
                    S.dma(xt[b], xsrc[tok0:tok0 + 128, :], writes=[t_xt[b]])
                    for n4 in range(4):
                        bk = 4 + n4
                        for kc in range(KC):
                            S.op("pe", lambda e, bk=bk, kc=kc, i=i, mb=mb, n4=n4: e.matmul(bank(bk), lhsT=mx[mb][:, kc, i * 128:(i + 1) * 128],
                                                                                            rhs=wo[:, kc, n4 * 512:(n4 + 1) * 512],
                                                                                            start=(kc == 0), stop=(kc == KC - 1)),
                                 reads=[t_mx[mb], t_wo], writes=[t_bk[n4]], inc=(kc == KC - 1))
                        S.op("dve", lambda e, bk=bk, n4=n4, b=b: e.tensor_tensor(out=x1[b][:, n4 * 512:(n4 + 1) * 512], in0=bank(bk),
                                                                                  in1=g1[:, n4 * 512:(n4 + 1) * 512], op=ALU.mult),
                             reads=[t_bk[n4], t_g1], writes=[t_x1[b]])
                    if pend[0] is not None:
                        pend[0]()
                        pend[0] = None
                        if pstore[0] is not None:
                            pstore[0]()
                            pstore[0] = None
                    S.op("dve", lambda e, b=b: e.tensor_tensor(out=x1[b], in0=x1[b], in1=xt[b], op=ALU.add), reads=[t_x1[b], t_xt[b]], writes=[t_x1[b]])
                    S.dma(x1s[tok0:tok0 + 128, :], x1[b], reads=[t_x1[b]])
                    pend[0] = norm_mod_transpose(n, x1[b], t_x1[b], gmT, shT, t_g,
                                                 lambda kc, i=i, mb=mb: h2[mb][:, kc, i * 128:(i + 1) * 128], lambda kc, i=i, mb=mb: t_h2[mb][kc][i],
                                                 defer=True)
                pstore[0] = (lambda tb=tb, mb=mb: S.dma(h2T[:, :, tb * TB:(tb + 1) * TB].rearrange("k p t -> p k t"), h2[mb],
                                                        reads=[t_h2[mb][kc][i] for kc in range(KC) for i in range(4)]))
            if pend[0] is not None:
                pend[0]()
                pend[0] = None
            if pstore[0] is not None:
                pstore[0]()
                pstore[0] = None
            if os.environ.get("KDEBUG"):
                print("arena words used", A.off, "of", AW)
            S.barrier()
            S.release()

        def phase_cb(l, last):
            A.reset(A_base)
            TB = 512
            NB = T // TB
            g2 = A.alloc([D])
            t_g2 = T_()
            bcast_row(g2, modscr, l * 6 * D + 5 * D, t_g2)
            if last:
                fn = A.alloc([D])
                t_fn = T_()
                bcast_row(fn, final_norm, 0, t_fn)
                n = make_norm_ctx()
            fw = A.alloc([4, 48])
            t_fw = T_()
            for j in range(3):
                S.dma(fw[:, j, :], ffn_conv_w[l, j, :].rearrange("(mc p) -> p mc", p=128), writes=[t_fw], allow_slow_non_contiguous=True)
            S.dma(fw[:, 3, :], ffn_conv_b[l, :].rearrange("(mc p) -> p mc", p=128), writes=[t_fw], allow_slow_non_contiguous=True)
            hb = A.alloc([KC, TB], BF16)
            t_hb = T_()
            halo = A.alloc([KC, 2], BF16)
            t_halo = T_()
            wgv = [A.alloc([2, KC, 128], BF16) for _ in range(2)]
            t_wgv = [T_(), T_()]
            actT = A.alloc([48, TB], BF16)
            t_act = [T_() for _ in range(48)]
            gsb = [A.alloc([TB + 2]) for _ in range(2)]
            t_gsb = [T_(), T_()]
            cac = [A.alloc([TB]) for _ in range(2)]
            t_cac = [T_(), T_()]
            sg = [A.alloc([TB]) for _ in range(2)]
            t_sg = [T_(), T_()]
            vsb = [A.alloc([TB]) for _ in range(2)]
            t_vsb = [T_(), T_()]
            wd = [A.alloc([16, 512], BF16) for _ in range(2)]
            t_wd = [T_(), T_()]
            xr_ = [A.alloc([D]) for _ in range(4)]
            t_xr = [T_() for _ in range(4)]
            t_G = [T_(), T_()]
            t_Vp = T_()
            t_Hh = [T_(), T_()]
            t_acc = [T_() for _ in range(4)]
            tmp = sg
            t_tmp = t_sg
            wcnt = 0
            dcnt = 0
            ecnt = 0
            for tb in range(NB):
                tok0 = tb * TB
                S.dma(hb, h2T[:, :, tok0:tok0 + TB].rearrange("k p t -> p k t"), writes=[t_hb])
                if tb == 0:
                    S.op("dve", lambda e: e.memset(halo[:, :, 0:1], 0.0), writes=[t_halo])
                else:
                    S.dma(halo[:, :, 0:1], h2T[:, :, tok0 - 1:tok0].rearrange("k p t -> p k t"), writes=[t_halo], allow_slow_non_contiguous=True)
                if tb == NB - 1:
                    S.op("dve", lambda e: e.memset(halo[:, :, 1:2], 0.0), writes=[t_halo])
                else:
                    S.dma(halo[:, :, 1:2], h2T[:, :, tok0 + TB:tok0 + TB + 1].rearrange("k p t -> p k t"), writes=[t_halo], allow_slow_non_contiguous=True)
                for i in range(4):
                    S.dma(xr_[i], x1s[tok0 + i * 128:tok0 + (i + 1) * 128, :], writes=[t_xr[i]])
                for mc in range(48):
                    wbi = wcnt % 2
                    wcnt += 1
                    S.dma(wgv[wbi], wup_bf[l, mc].rearrange("g p k n -> p g k n"), writes=[t_wgv[wbi]], reads=[t_wc[("wup", l)]])
                    gb = mc % 2
                    Gb = bank(4 + gb)
                    Vp = bank(6)
                    Hh = bank(7)[:, 0:2]
                    for kc in range(KC):
                        S.op("pe", lambda e, Gb=Gb, kc=kc, wbi=wbi: e.matmul(Gb, lhsT=wgv[wbi][:, 0, kc, :], rhs=hb[:, kc, :], start=(kc == 0), stop=(kc == KC - 1)),
                             reads=[t_wgv[wbi], t_hb], writes=[t_G[gb]], inc=(kc == KC - 1))
                    for kc in range(KC):
                        S.op("pe", lambda e, Hh=Hh, kc=kc, wbi=wbi: e.matmul(Hh, lhsT=wgv[wbi][:, 0, kc, :], rhs=halo[:, kc, :], start=(kc == 0), stop=(kc == KC - 1)),
                             reads=[t_wgv[wbi], t_halo], writes=[t_Hh[0]], inc=(kc == KC - 1))
                    for kc in range(KC):
                        S.op("pe", lambda e, Vp=Vp, kc=kc, wbi=wbi: e.matmul(Vp, lhsT=wgv[wbi][:, 1, kc, :], rhs=hb[:, kc, :], start=(kc == 0), stop=(kc == KC - 1)),
                             reads=[t_wgv[wbi], t_hb], writes=[t_Vp], inc=(kc == KC - 1))
                    eb = ecnt % 2
                    ecnt += 1
                    S.op("act", lambda e, eb=eb, Gb=Gb: e.activation(out=gsb[eb][:, 1:TB + 1], in_=Gb, func=AF.Copy), reads=[t_G[gb]], writes=[t_gsb[eb]])
                    S.op("act", lambda e, eb=eb, Hh=Hh: e.activation(out=gsb[eb][:, 0:1], in_=Hh[:, 0:1], func=AF.Copy), reads=[t_Hh[0]], writes=[t_gsb[eb]])
                    S.op("act", lambda e, eb=eb, Hh=Hh: e.activation(out=gsb[eb][:, TB + 1:TB + 2], in_=Hh[:, 1:2], func=AF.Copy), reads=[t_Hh[0]], writes=[t_gsb[eb]])
                    S.op("dve", lambda e, eb=eb, mc=mc: e.tensor_scalar(out=cac[eb], in0=gsb[eb][:, 0:TB], scalar1=fw[:, 0, mc:mc + 1], scalar2=fw[:, 3, mc:mc + 1],
                                                                        op0=ALU.mult, op1=ALU.add),
                         reads=[t_gsb[eb], t_fw], writes=[t_cac[eb]])
                    for j in (1, 2):
                        S.op("dve", lambda e, eb=eb, mc=mc, j=j: e.scalar_tensor_tensor(out=cac[eb], in0=gsb[eb][:, j:j + TB], scalar=fw[:, j, mc:mc + 1], in1=cac[eb],
                                                                                        op0=ALU.mult, op1=ALU.add),
                             reads=[t_gsb[eb], t_fw, t_cac[eb]], writes=[t_cac[eb]])
                    S.op("act", lambda e, eb=eb: e.activation(out=sg[eb], in_=cac[eb], func=AF.Gelu_apprx_tanh), reads=[t_cac[eb]], writes=[t_sg[eb]])
                    S.op("act", lambda e, eb=eb, Vp=Vp: e.activation(out=vsb[eb], in_=Vp, func=AF.Copy), reads=[t_Vp], writes=[t_vsb[eb]])
                    S.op("dve", lambda e, eb=eb, mc=mc: e.tensor_tensor(out=actT[:, mc, :], in0=vsb[eb], in1=sg[eb], op=ALU.mult),
                         reads=[t_vsb[eb], t_sg[eb]], writes=[t_act[mc]])
                for nb in range(4):
                    for k3 in range(3):
                        wdi = dcnt % 2
                        dcnt += 1
                        S.dma(wd[wdi], wdn_bf[l, nb][:, k3 * 16:(k3 + 1) * 16, :], writes=[t_wd[wdi]], reads=[t_wc[("wdn", l)]])
                        for i in range(4):
                            for kk in range(16):
                                kc = k3 * 16 + kk
                                S.op("pe", lambda e, i=i, kc=kc, kk=kk, wdi=wdi: e.matmul(bank(i), lhsT=actT[:, kc, i * 128:(i + 1) * 128], rhs=wd[wdi][:, kk, :],
                                                                                          start=(kc == 0), stop=(kc == 47)),
                                     reads=[t_act[kc], t_wd[wdi]], writes=[t_acc[i]], inc=(kk == 15))
                    for i in range(4):
                        tq = (nb * 4 + i) % 2
                        S.op("dve", lambda e, i=i, nb=nb, tq=tq: e.tensor_tensor(out=tmp[tq], in0=bank(i), in1=g2[:, nb * 512:(nb + 1) * 512], op=ALU.mult),
                             reads=[t_acc[i], t_g2], writes=[t_tmp[tq]])
                        S.op("pool", lambda e, i=i, nb=nb, tq=tq: e.tensor_tensor(out=xr_[i][:, nb * 512:(nb + 1) * 512], in0=xr_[i][:, nb * 512:(nb + 1) * 512],
                                                                                   in1=tmp[tq], op=ALU.add),
                             reads=[t_tmp[tq], t_xr[i]], writes=[t_xr[i]])
                for i in range(4):
                    r0 = tok0 + i * 128
                    if last:
                        k = n.cnt % 4
                        n.cnt += 1
                        rstd_of(n, xr_[i], t_xr[i], k)
                        S.op("dve", lambda e, i=i, k=k: e.scalar_tensor_tensor(out=xr_[i], in0=xr_[i], scalar=n.rs[k], in1=fn, op0=ALU.mult, op1=ALU.mult),
                             reads=[t_xr[i], n.t_rs[k], t_fn], writes=[t_xr[i]])
                        S.dma(y_out[r0:r0 + 128, :], xr_[i], reads=[t_xr[i]])
                    else:
                        S.dma(xls[r0:r0 + 128, :], xr_[i], reads=[t_xr[i]])
            if os.environ.get("KDEBUG"):
                print("arena words used", A.off, "of", AW)
            S.barrier()
            S.release()

        import os
        stop = os.environ.get("KSTOP", "")
        prologue()
        for l in range(NL):
            if stop == "pro":
                break
            xsrc = x_in if l == 0 else xls
            phase_a(l, xsrc)
            if stop == "a":
                break
            phase_attn(l)
            if stop == "attn1":
                break
            phase_rnn(l)
            if stop in ("attn1", "rnn1", "b"):
                break
            phase_ca(l, xsrc)
            if stop == "ca":
                break
            phase_cb(l, l == NL - 1)
        S.barrier(final=True)
        S.emit()
    return nc


_CACHE = {}


def _consts():
    return {"ident": np.eye(128, dtype=np.float32).astype(ml_dtypes.bfloat16), "onehot": make_onehot()}


def make_in_map(x_seq, c_seq, weights):
    m = dict(weights)
    m["x"] = np.ascontiguousarray(x_seq, dtype=np.float32)
    m["c"] = np.ascontiguousarray(c_seq, dtype=np.float32).reshape(1, D)
    m.update(_consts())
    return m


def kernel(x_prompt, x_sample, c_prompt, c_sample, w_ada, b_ada, norm1, w_in, w_out, rel_bias,
           rnn_conv_w, rnn_conv_b, w_rg, b_rg, w_ig, b_ig, lam, norm2, w_up, ffn_conv_w, ffn_conv_b,
           w_down, final_norm):
    T = x_prompt.shape[1]
    weights = {"w_ada": w_ada, "b_ada": b_ada, "norm1": norm1, "w_in": w_in, "w_out": w_out, "rel_bias": rel_bias,
               "rnn_conv_w": rnn_conv_w, "rnn_conv_b": rnn_conv_b, "w_rg": w_rg, "b_rg": b_rg, "w_ig": w_ig,
               "b_ig": b_ig, "lam": lam, "norm2": norm2, "w_up": w_up, "ffn_conv_w": ffn_conv_w,
               "ffn_conv_b": ffn_conv_b, "w_down": w_down, "final_norm": np.asarray(final_norm).reshape(1, D)}
    weights = {k: np.ascontiguousarray(np.asarray(v), dtype=np.float32) for k, v in weights.items()}
    xs = [np.asarray(x_prompt)[0]] + [np.asarray(x_sample)[i] for i in range(4)]
    cs = [np.asarray(c_prompt)[0]] + [np.asarray(c_sample)[i] for i in range(4)]
    seq_of_core = [0, 1, 2, 3, 4, 0, 1, 2]
    if "nc" not in _CACHE:
        _CACHE["nc"] = build(T=T, NL=2)
    nc = _CACHE["nc"]
    in_maps = [make_in_map(xs[s], cs[s], weights) for s in seq_of_core]
    res = run_bass_kernel_spmd(nc, in_maps, core_ids=list(range(8)))
    outs = [np.asarray(res.results[i]["y"], dtype=np.float32) for i in range(5)]
    y_prompt = outs[0][None]
    y_sample = np.stack(outs[1:5], axis=0)
    return (y_prompt, y_sample)
```
